# Optimizing a Trainium2 kernel written in Bass

```python
import jax, jax.numpy as jnp
from jax import lax
import numpy as np

D_MODEL = 1024
BATCH = 8
SEQ = 2048
DEPTH = 2
DEC_BATCH = 128
DEC_SEQ = 8
PAST_LEN = 16384
PAGE_SIZE = 128

CHUNK = 128
NH_A = 4
HD_A = 128
D_A = NH_A * HD_A
NH_B = 8
HD_B = 64
D_B = NH_B * HD_B
CONV_W = 3
D_IN_EVEN = 2 * D_A + 3 * D_B
POOL_WINDOWS = (2, 4, 8, 16)
N_POOL_GROUPS = len(POOL_WINDOWS)
POOL_GC = D_MODEL // N_POOL_GROUPS
POOL_CTX = max(POOL_WINDOWS) - 1
D_FF = 128 * ((8 * D_MODEL // 3 + 127) // 128)
N_EVEN = (DEPTH + 1) // 2
N_ODD = DEPTH // 2
EPS = 1e-6

kernel_name = 'hybrid_sgu_shortconv_pool_macaron_step'


def _rmsnorm(x, g):
    xf = x.astype(jnp.float32)
    y = xf * lax.rsqrt(jnp.mean(xf * xf, axis=-1, keepdims=True) + EPS) * g.astype(jnp.float32)
    return y.astype(x.dtype)


def _layernorm(x, g, b):
    xf = x.astype(jnp.float32)
    mu = jnp.mean(xf, axis=-1, keepdims=True)
    var = jnp.mean(jnp.square(xf - mu), axis=-1, keepdims=True)
    y = (xf - mu) * lax.rsqrt(var + EPS) * g.astype(jnp.float32) + b.astype(jnp.float32)
    return y.astype(x.dtype)


def _swiglu(h, w_gate, w_up, w_down):
    return (jax.nn.silu(h @ w_gate) * (h @ w_up)) @ w_down


def _sgu(u, v, w_s, b_s, chunk_len):
    b, L, _ = v.shape
    n_c = L // chunk_len
    mask = jnp.tril(jnp.ones((chunk_len, chunk_len), dtype=bool))
    w = jnp.where(mask[None], w_s[:, :chunk_len, :chunk_len], 0.0).astype(v.dtype)
    vc = v.reshape(b, n_c, chunk_len, NH_A, HD_A)
    mixed = jnp.einsum('hts,bcshd->bcthd', w, vc) + b_s[:, :chunk_len].T[None, None, :, :, None]
    return u * mixed.reshape(b, L, D_A)


def _even_mixer(h, conv_buf, chunk_len, w_in, ln_g, ln_b, sgu_w, sgu_b, conv_w, w_out):
    L = h.shape[1]
    z = h @ w_in
    uv = jax.nn.gelu(z[..., :2 * D_A], approximate=False)
    u = uv[..., :D_A]
    v = _layernorm(uv[..., D_A:], ln_g, ln_b)
    y_a = _sgu(u, v, sgu_w, sgu_b, chunk_len)
    o = 2 * D_A
    gate_b = z[..., o:o + D_B]
    gate_c = z[..., o + D_B:o + 2 * D_B]
    x_in = z[..., o + 2 * D_B:o + 3 * D_B]
    xg = gate_c * x_in
    ext = jnp.concatenate([conv_buf.astype(xg.dtype), xg], axis=1)
    conv = ext[:, 0:L] * conv_w[0]
    for k in range(1, CONV_W):
        conv = conv + ext[:, k:k + L] * conv_w[k]
    y_b = gate_b * conv
    out = jnp.concatenate([y_a, y_b], axis=-1) @ w_out
    return out, ext[:, -(CONV_W - 1):], v


def _odd_mixer(h, pool_buf, pos, w_in, pool_w, pool_scale, w_out):
    b, T, _ = h.shape
    p = h @ w_in
    ext = jnp.concatenate([pool_buf.astype(p.dtype), p], axis=1)
    c = jnp.cumsum(ext.astype(jnp.float32), axis=1)
    c = jnp.concatenate([jnp.zeros_like(c[:, :1]), c], axis=1)
    end = c[:, POOL_CTX + 1:]
    pf = p.astype(jnp.float32)
    outs = []
    for g, w in enumerate(POOL_WINDOWS):
        sl = slice(g * POOL_GC, (g + 1) * POOL_GC)
        start = c[:, POOL_CTX + 1 - w:POOL_CTX + 1 - w + T, sl]
        cnt = jnp.minimum(pos + 1, w).astype(jnp.float32)[None, :, None]
        outs.append((end[..., sl] - start) / cnt - pf[..., sl])
    d = jnp.stack(outs, axis=2).astype(p.dtype)
    y = jnp.einsum('btgc,gcd->btgd', d, pool_w).reshape(b, T, D_MODEL) * pool_scale
    return y @ w_out, ext[:, -POOL_CTX:]


def _trunk(x, conv_bufs, pool_bufs, pos, chunk_len, keep_v,
           norm_g, final_norm_g, ffn_w_gate, ffn_w_up, ffn_w_down,
           e_w_in, e_ln_g, e_ln_b, e_sgu_w, e_sgu_b, e_conv_w, e_w_out,
           o_w_in, o_pool_w, o_pool_scale, o_w_out):
    new_conv, new_pool, v_rows = [], [], []
    for layer in range(DEPTH):
        i = layer // 2
        x = x + 0.5 * _swiglu(_rmsnorm(x, norm_g[layer, 0]),
                              ffn_w_gate[layer, 0], ffn_w_up[layer, 0], ffn_w_down[layer, 0])
        h = _rmsnorm(x, norm_g[layer, 1])
        if layer % 2 == 0:
            m, cb, v = _even_mixer(h, conv_bufs[i], chunk_len, e_w_in[i], e_ln_g[i], e_ln_b[i],
                                   e_sgu_w[i], e_sgu_b[i], e_conv_w[i], e_w_out[i])
            new_conv.append(cb)
            v_rows.append(v)
        else:
            m, pb = _odd_mixer(h, pool_bufs[i], pos, o_w_in[i], o_pool_w[i], o_pool_scale[i], o_w_out[i])
            new_pool.append(pb)
        x = x + m
        x = x + 0.5 * _swiglu(_rmsnorm(x, norm_g[layer, 2]),
                              ffn_w_gate[layer, 1], ffn_w_up[layer, 1], ffn_w_down[layer, 1])
    y = _rmsnorm(x, final_norm_g)
    v_out = jnp.stack(v_rows) if keep_v else None
    return y, jnp.stack(new_conv), v_out, jnp.stack(new_pool)


def setup_inputs(seed: int = 0) -> dict:
    key = jax.random.key(seed)
    ks = jax.random.split(key, 24)
    f32 = jnp.float32
    nrm = lambda k, shape, s: (jax.random.normal(k, shape, f32) * s).astype(f32)
    return {
        'x_prompt': nrm(ks[0], (BATCH, SEQ, D_MODEL), 1.0),
        'x_sample': nrm(ks[1], (DEC_BATCH, DEC_SEQ, D_MODEL), 1.0),
        'state_conv': nrm(ks[2], (N_EVEN, DEC_BATCH, CONV_W - 1, D_B), 0.5),
        'state_pool': nrm(ks[3], (N_ODD, DEC_BATCH, POOL_CTX, D_MODEL), 0.5),
        'norm_g': 1.0 + nrm(ks[4], (DEPTH, 3, D_MODEL), 0.02),
        'final_norm_g': 1.0 + nrm(ks[5], (D_MODEL,), 0.02),
        'ffn_w_gate': nrm(ks[6], (DEPTH, 2, D_MODEL, D_FF), D_MODEL ** -0.5),
        'ffn_w_up': nrm(ks[7], (DEPTH, 2, D_MODEL, D_FF), D_MODEL ** -0.5),
        'ffn_w_down': nrm(ks[8], (DEPTH, 2, D_FF, D_MODEL), D_FF ** -0.5),
        'e_w_in': nrm(ks[9], (N_EVEN, D_MODEL, D_IN_EVEN), D_MODEL ** -0.5),
        'e_ln_g': 1.0 + nrm(ks[10], (N_EVEN, D_A), 0.02),
        'e_ln_b': nrm(ks[11], (N_EVEN, D_A), 0.02),
        'e_sgu_w': nrm(ks[12], (N_EVEN, NH_A, CHUNK, CHUNK), CHUNK ** -0.5),
        'e_sgu_b': 1.0 + nrm(ks[13], (N_EVEN, NH_A, CHUNK), 0.02),
        'e_conv_w': nrm(ks[14], (N_EVEN, CONV_W, D_B), CONV_W ** -0.5),
        'e_w_out': nrm(ks[15], (N_EVEN, D_A + D_B, D_MODEL), (D_A + D_B) ** -0.5),
        'o_w_in': nrm(ks[16], (N_ODD, D_MODEL, D_MODEL), D_MODEL ** -0.5),
        'o_pool_w': nrm(ks[17], (N_ODD, N_POOL_GROUPS, POOL_GC, POOL_GC), POOL_GC ** -0.5),
        'o_pool_scale': 1.0 + nrm(ks[18], (N_ODD, D_MODEL), 0.02),
        'o_w_out': nrm(ks[19], (N_ODD, D_MODEL, D_MODEL), D_MODEL ** -0.5),
    }


def reference(x_prompt, x_sample, state_conv, state_pool, norm_g, final_norm_g,
              ffn_w_gate, ffn_w_up, ffn_w_down, e_w_in, e_ln_g, e_ln_b, e_sgu_w, e_sgu_b,
              e_conv_w, e_w_out, o_w_in, o_pool_w, o_pool_scale, o_w_out):
    weights = (norm_g, final_norm_g, ffn_w_gate, ffn_w_up, ffn_w_down,
               e_w_in, e_ln_g, e_ln_b, e_sgu_w, e_sgu_b, e_conv_w, e_w_out,
               o_w_in, o_pool_w, o_pool_scale, o_w_out)
    pos_p = jnp.arange(SEQ, dtype=jnp.int32)
    conv0 = jnp.zeros((N_EVEN, BATCH, CONV_W - 1, D_B), x_prompt.dtype)
    pool0 = jnp.zeros((N_ODD, BATCH, POOL_CTX, D_MODEL), x_prompt.dtype)
    y_prompt, conv_prompt, _, pool_prompt = _trunk(
        x_prompt, conv0, pool0, pos_p, CHUNK, False, *weights)
    pos_s = PAST_LEN + jnp.arange(DEC_SEQ, dtype=jnp.int32)
    y_sample, conv_sample, chunk_v_sample, pool_sample = _trunk(
        x_sample, state_conv, state_pool, pos_s, DEC_SEQ, True, *weights)
    return (y_prompt, y_sample, conv_prompt, conv_sample, chunk_v_sample, pool_prompt, pool_sample)
```

```python
import numpy as np
import concourse.bass as bass
import concourse.mybir as mybir
from concourse.bass_utils import run_bass_kernel_spmd

F32 = mybir.dt.float32
BF16 = mybir.dt.bfloat16
AF = mybir.ActivationFunctionType
ALU = mybir.AluOpType

NCORES = 8
_DBG = {}
D = 1024
NCH = 8
FF = 2816
NF = 22
SEQ = 2048
EPS = 1e-6
TMAX = 1152
NBIG = 14080
NSLOT = 4
SLOT_E = 4096
SAME_ENG_SYNC = True

SBS = [
    dict(idx=0, Tp=1024, Ts=0, T=1024, poff=0, blocks=[(0, 512), (512, 512)]),
    dict(idx=1, Tp=1024, Ts=128, T=1152, poff=1024, blocks=[(0, 384), (384, 384), (768, 384)]),
]


class Op:
    __slots__ = ("eng", "fn", "deps", "ndma", "sem", "val", "needed", "prev_val")

    def __init__(self, eng, fn, deps, ndma):
        self.eng, self.fn, self.deps, self.ndma = eng, fn, deps, ndma
        self.sem = None
        self.val = 0
        self.prev_val = 0
        self.needed = False


class Prog:
    ENGS = ("pe", "act", "dve", "pool", "sp")

    def __init__(self):
        self.ops = []
        self.last_w = {}
        self.readers = {}
        self.dry = False
        self.witems = []
        self.big_role = None
        self.big_cur = {}
        self.big_prev = {}

    def add(self, eng, fn, r=(), w=(), ndma=0):
        if self.dry:
            return -1
        idx = len(self.ops)
        deps = set()
        touches_big = False
        for k in r:
            if k[0] == "big":
                touches_big = True
                self._big_role(k[1])
            d = self.last_w.get(k)
            if d is not None:
                deps.add(d)
        for k in w:
            if k[0] == "big":
                touches_big = True
                self._big_role(k[1])
            d = self.last_w.get(k)
            if d is not None:
                deps.add(d)
            deps.update(self.readers.get(k, ()))
        if touches_big:
            deps.update(self.big_prev.values())
            self.big_cur[eng] = idx
        for k in r:
            self.readers.setdefault(k, []).append(idx)
        for k in w:
            self.last_w[k] = idx
            self.readers[k] = []
        deps.discard(idx)
        self.ops.append(Op(eng, fn, deps, ndma))
        return idx

    def _big_role(self, role):
        if role != self.big_role:
            for e, i in self.big_cur.items():
                self.big_prev[e] = max(self.big_prev.get(e, -1), i)
            self.big_cur = {}
            self.big_role = role

    def emit(self, nc, block, sems, rings):
        ops = self.ops
        for op in ops:
            for d in op.deps:
                ops[d].needed = True
        cnt = {e: 0 for e in self.ENGS}
        ring_tot = {q: [0] * len(rings[q]) for q in rings}
        ring_n = {q: 0 for q in rings}
        for op in ops:
            if op.ndma:
                q = op.eng
                j = ring_n[q] % len(rings[q])
                ring_n[q] += 1
                op.sem = rings[q][j]
                op.prev_val = ring_tot[q][j]
                ring_tot[q][j] += 16 * op.ndma
                op.val = ring_tot[q][j]
            elif op.needed:
                cnt[op.eng] += 1
                op.sem = sems[op.eng]
                op.val = cnt[op.eng]

        def run_engine(ename, e):
            waited = {}
            for op in ops:
                if op.eng != ename:
                    continue
                want = {}
                for d in op.deps:
                    dop = ops[d]
                    if dop.ndma == 0 and dop.eng == ename:
                        if ename == "pe" or not SAME_ENG_SYNC:
                            continue
                    key = id(dop.sem)
                    if key not in want or want[key][1] < dop.val:
                        want[key] = (dop.sem, dop.val)
                if op.ndma and op.prev_val > 0:
                    key = id(op.sem)
                    if key not in want or want[key][1] < op.prev_val:
                        want[key] = (op.sem, op.prev_val)
                for key, (sem, val) in want.items():
                    if waited.get(key, 0) >= val:
                        continue
                    e.wait_ge(sem, val)
                    waited[key] = val
                if op.fn is None:
                    continue
                ins = op.fn(e)
                if op.ndma:
                    assert len(ins) == op.ndma
                    for i_ in ins:
                        i_.then_inc(op.sem, 16)
                elif op.needed:
                    ins.then_inc(op.sem, 1)

        @block.tensor
        def _(e):
            run_engine("pe", e)

        @block.scalar
        def _(e):
            run_engine("act", e)

        @block.vector
        def _(e):
            run_engine("dve", e)

        @block.gpsimd
        def _(e):
            run_engine("pool", e)

        @block.sync
        def _(e):
            run_engine("sp", e)


def tiles_of(a, n):
    return range(a // 128, (a + n + 127) // 128)


def K(name, cs, a, n):
    if isinstance(cs, int):
        cs = (cs,)
    return [(name, c, t) for c in cs for t in tiles_of(a, n)]


ALLC = tuple(range(NCH))


def build_program():
    nc = bass.Bass("TRN2", target_bir_lowering=False)

    def din(name, shape):
        return nc.dram_tensor(name, list(shape), F32, kind="ExternalInput").ap()

    def dout(name, shape):
        return nc.dram_tensor(name, list(shape), F32, kind="ExternalOutput").ap()

    xp = din("xp", [SEQ, D]); xs = din("xs", [128, D])
    sc = din("sc", [32, 512]); spl = din("spl", [240, D])
    ng = din("ng", [48, 128]); fg = din("fg", [8, 128])
    wg = din("wg", [2, 2, D, FF]); wu = din("wu", [2, 2, D, FF]); wd = din("wd", [2, 2, FF, D])
    ewin = din("ewin", [D, 2560]); lng = din("lng", [1, 512]); lnb = din("lnb", [1, 512])
    sguw = din("sguw", [4, 128, 128]); sgub = din("sgub", [4, 128])
    convw = din("convw", [12, 128]); ewout = din("ewout", [D, D])
    owin = din("owin", [D, D]); poolw = din("poolw", [4, 256, 256]); pscale = din("pscale", [8, 128])
    owout = din("owout", [D, D])

    yp = dout("yp", [SEQ, D]); ys = dout("ys", [128, D])
    cvp = dout("cvp", [2, 512]); cvs = dout("cvs", [32, 512]); cvv = dout("cvv", [128, 512])
    plp = dout("plp", [15, D]); pls = dout("pls", [16, 15, D])

    ewin_v = ewin.rearrange("(c p) j -> p c j", p=128)
    ewout_v = ewout.rearrange("(c p) j -> p c j", p=128)
    owin_v = owin.rearrange("(c p) j -> p c j", p=128)
    owout_v = owout.rearrange("(c p) j -> p c j", p=128)
    poolw_v = poolw.rearrange("g (k p) o -> p (g k) o", p=128)

    from contextlib import ExitStack
    es = ExitStack()

    def sb(name, shape, dt=F32):
        return es.enter_context(nc.sbuf_tensor(name, list(shape), dt))

    xT = sb("xT", [128, NCH, TMAX])
    hT = sb("hT", [128, NCH, TMAX], BF16)
    mixT = sb("mixT", [128, NCH, TMAX], BF16)
    big = sb("big", [128, NBIG])
    wslots = [sb(f"wslot{i}", [128, SLOT_E], BF16) for i in range(NSLOT)]
    stage = [sb(f"stage{i}", [128, 1024]) for i in range(2)]
    stmp = [sb(f"stmp{i}", [128, 512]) for i in range(2)]
    rt = sb("rt", [128, 512]); rs = sb("rs", [128, 512])
    lng_bc = sb("lng_bc", [128, 512]); lnb_bc = sb("lnb_bc", [128, 512])
    vbf_all = sb("vbf_all", [128, 9, 512], BF16)
    vfp = stmp[0]
    sqbuf = sb("sqbuf", [128, 2304], BF16)
    bsb_p = sb("bsb_p", [128, 512]); bsb_s = sb("bsb_s", [128, 512])
    WTp = sb("WTp", [128, 4, 128], BF16); WTs = sb("WTs", [128, 4, 128], BF16)
    ident = sb("ident", [128, 128])
    onesm = sb("onesm", [128, 128], BF16)
    cT = sb("cT", [128, 76])
    RC = sb("RC", [128, 4, 16])
    misc = sb("misc", [128, 192])
    Csg = sb("Csg", [128, 4, 8])
    convst = sb("convst", [128, 4, 2])
    cvs_buf = sb("cvs_buf", [128, 4, 16, 2])
    poolst = sb("poolst", [128, NCH, 15])
    scT = sb("scT", [128, 4, 16, 2])
    tmp16 = sb("tmp16", [128, 16])
    banks = [es.enter_context(nc.psum_tensor(f"ps{i}", [128, 512], F32)) for i in range(8)]

    bigbf = big[:, 0:NF * TMAX // 2].bitcast(BF16)
    aT = bigbf.rearrange("p (f t) -> p f t", f=NF)
    vg_all = big[:, 0:4608].rearrange("p (i j) -> p i j", i=9)
    uT = big[:, 4608:9216].rearrange("p (c t) -> p c t", c=4)
    o = 9216
    xin_s = big[:, o:o + 1152]; o += 1152
    ext = big[:, o:o + 1028]; o += 1028
    ext_s = big[:, o:o + 160].rearrange("p (b k) -> p b k", b=16); o += 160
    gb_s = big[:, o:o + 1152]; o += 1152
    acc = big[:, o:o + 1152]; o += 1152
    assert o <= NBIG
    p_ext = big[:, 0:8320].rearrange("p (c t) -> p c t", c=8)
    ps_ext = big[:, 8320:8320 + 2944].rearrange("p (c b k) -> p c b k", c=8, b=16)
    o = 8320 + 2944
    t1 = big[:, o:o + 1040]; o += 1040
    t2 = big[:, o:o + 1040]; o += 1040
    ts1 = big[:, o:o + 368].rearrange("p (b k) -> p b k", b=16); o += 368
    ts2 = big[:, o:o + 368].rearrange("p (b k) -> p b k", b=16); o += 368
    assert o <= NBIG
    yT = big[:, 0:NCH * TMAX].rearrange("p (c t) -> p c t", c=NCH)

    stats = misc[:, 0:54].rearrange("p (i s) -> p i s", i=9)
    mv = misc[:, 54:72].rearrange("p (i s) -> p i s", i=9)
    rl = misc[:, 72:81]
    rl2 = misc[:, 81:90]
    ssq = misc[:, 128:146]
    fsum = misc[:, 146:155]
    fsq = misc[:, 155:164]
    frs = misc[:, 164:173]

    P = Prog()

    wstate = dict(n=0)

    def witem(loads):
        i = wstate["n"]
        wstate["n"] += 1
        if P.dry:
            P.witems.append(loads)
            return i % NSLOT
        j = i + NSLOT - 2
        if j < len(P.witems):
            _emit_wload(j)
        return i % NSLOT

    def _emit_wload(j, extra_r=()):
        loads = P.witems[j]
        s = j % NSLOT
        slot = wslots[s]

        def fn(e, loads=loads, slot=slot):
            res = []
            for (off, shp, src) in loads:
                n = 1
                for v in shp:
                    n *= v
                dst = slot[:, off:off + n]
                if len(shp) == 2:
                    dst = dst.rearrange("p (a b) -> p a b", a=shp[0])
                res.append(e.dma_start(out=dst, in_=src))
            return res
        P.add("pool", fn, r=list(extra_r), w=[("ws", s)], ndma=len(loads))

    def wprologue():
        if P.dry:
            return
        for j in range(min(NSLOT - 2, len(P.witems))):
            _emit_wload(j, extra_r=[("big", "load", "st", SBS[0]["T"] // 128 - 1)])

    def slot_view(s, off, a, b):
        return wslots[s][:, off:off + a * b].rearrange("p (a b) -> p a b", a=a)

    def emit_all():
        wstate["n"] = 0
        prologue_a()
        load_dma(SBS[0])
        wprologue()
        for S in SBS:
            load_rest(S)
            if S["idx"] == 0:
                prologue_b()
            tl = None
            for layer in range(2):
                if layer == 1 and S["Ts"]:
                    odd_prefetch(S)
                tl = ffn(S, layer, 0, (layer * 3 + 1, "h"), tl)
                if layer == 0:
                    tl = even_mixer(S, tl)
                else:
                    tl = odd_mixer(S, tl)
                tl = ffn(S, layer, 1, (3, "h") if layer == 0 else None, tl)
            if S["idx"] == 0:
                load_dma(SBS[1])
            final_out(S)
        P.add("sp", None, r=list(out_keys))

    out_keys = []

    def okey(name):
        k = ("out", name, len(out_keys))
        out_keys.append(k)
        return k

    def prologue_a():
        P.add("pool", lambda e: e.memset(stage[1][:, 0:128], 1.0), w=[("stage", 1)])
        P.add("pool", lambda e: e.affine_select(out=ident[:], in_=stage[1][:, 0:128], pattern=[[1, 128]],
                                                compare_op=ALU.is_equal, fill=0.0, base=0, channel_multiplier=-1),
              r=[("stage", 1)], w=[("ident",)])
        P.add("pool", lambda e: e.memset(onesm[:], 1.0 / 1024.0), w=[("onesm",)])
        P.add("pool", lambda e: e.iota(RC[:], pattern=[[0, 4], [1, 16]], base=1, channel_multiplier=0,
                                       allow_small_or_imprecise_dtypes=True), w=[("RC",)])
        for g in range(4):
            P.add("dve", lambda e, g=g: e.tensor_scalar_min(out=RC[:, g, :], in0=RC[:, g, :], scalar1=float(2 << g)),
                  r=[("RC",)], w=[("RC",)])
        P.add("dve", lambda e: e.reciprocal(out=RC[:], in_=RC[:]), r=[("RC",)], w=[("RC",)])
        def ld_c(e):
            return [e.dma_start(out=stage[0][0:48, 0:128], in_=ng),
                    e.dma_start(out=stage[0][48:56, 0:128], in_=fg),
                    e.dma_start(out=stage[0][56:68, 0:128], in_=convw),
                    e.dma_start(out=stage[0][68:76, 0:128], in_=pscale)]
        P.add("sp", ld_c, w=[("stage", 0)], ndma=4)
        P.add("pe", lambda e: e.transpose(banks[7][:, 0:76], stage[0][0:76, 0:128], ident[0:76, 0:76]),
              r=[("stage", 0), ("ident",)], w=[("ps", 7)])
        P.add("act", lambda e: e.activation(out=cT[:], in_=banks[7][:, 0:76], func=AF.Copy),
              r=[("ps", 7)], w=[("cT",)])
    def prologue_b():
        def ld_b(e):
            return [e.dma_start(out=lng_bc[:].unsqueeze(1), in_=lng.partition_broadcast(128)),
                    e.dma_start(out=lnb_bc[:].unsqueeze(1), in_=lnb.partition_broadcast(128)),
                    e.dma_start(out=bsb_p[:].unsqueeze(1), in_=sgub.rearrange("(o h) t -> o (h t)", o=1).partition_broadcast(128))]
        P.add("sp", ld_b, w=[("lnbc",), ("bs_p",)], ndma=3)
        for h in range(4):
            P.add("pool", lambda e, h=h: e.tensor_copy(
                out=bsb_s[:, h * 128:(h + 1) * 128].rearrange("p (b t) -> p b t", b=16),
                in_=bsb_p[:, h * 128:h * 128 + 8].unsqueeze(1).to_broadcast([128, 16, 8])),
                r=[("bs_p",)], w=[("bs_s",)])
        wn = stage[1][:, 0:512].rearrange("p (h s) -> p h s", h=4)
        P.add("sp", lambda e: [e.dma_start(out=wn, in_=sguw.rearrange("h t s -> t h s"))],
              r=[("ident",)], w=[("stage", 1)], ndma=1)
        P.add("pool", lambda e: e.affine_select(out=wn, in_=wn, pattern=[[0, 4], [-1, 128]], compare_op=ALU.is_ge,
                                                fill=0.0, base=0, channel_multiplier=1),
              r=[("stage", 1)], w=[("stage", 1)])
        def tr_p(e):
            for h in range(4):
                ins = e.transpose(banks[4][:, h * 128:(h + 1) * 128], wn[:, h, :], ident[:])
            return ins
        P.add("pe", tr_p, r=[("stage", 1), ("ident",)], w=[("ps", 4)])
        P.add("act", lambda e: e.activation(out=WTp[:], in_=banks[4][:].rearrange("p (h t) -> p h t", h=4), func=AF.Copy),
              r=[("ps", 4)], w=[("WTp",)])
        def ld_cs(e):
            res = []
            for j in range(16):
                res.append(e.dma_start(out=Csg[8 * j:8 * j + 8, :, :], in_=sguw[:, 0:8, 0:8].rearrange("h t s -> t h s")))
            return res
        P.add("sp", ld_cs, w=[("Csg",)], ndma=16)
        wns = stage[0][:, 512:1024].rearrange("p (h b s) -> p h b s", h=4, b=16)
        for h in range(4):
            P.add("pool", lambda e, h=h: e.affine_select(out=wns[:, h, :, :], in_=Csg[:, h, :].unsqueeze(1).to_broadcast([128, 16, 8]),
                                                    pattern=[[-8, 16], [-1, 8]], compare_op=ALU.is_ge, fill=0.0,
                                                    base=0, channel_multiplier=1),
                  r=[("Csg",), ("ps", 7)], w=[("stage", 0)])
        P.add("pool", lambda e: e.affine_select(out=wns, in_=wns, pattern=[[0, 4], [8, 16], [0, 8]], compare_op=ALU.is_ge,
                                                fill=0.0, base=7, channel_multiplier=-1),
              r=[("stage", 0)], w=[("stage", 0)])
        def tr_s(e):
            for h in range(4):
                ins = e.transpose(banks[5][:, h * 128:(h + 1) * 128],
                                  stage[0][:, 512 + h * 128:512 + (h + 1) * 128], ident[:])
            return ins
        P.add("pe", tr_s, r=[("stage", 0), ("ident",)], w=[("ps", 5)])
        P.add("act", lambda e: e.activation(out=WTs[:], in_=banks[5][:].rearrange("p (h t) -> p h t", h=4), func=AF.Copy),
              r=[("ps", 5)], w=[("WTs",)])

    def load_dma(S):
        nt = S["T"] // 128
        for i in range(nt):
            if i * 128 < S["Tp"]:
                src = xp[S["poff"] + i * 128:S["poff"] + (i + 1) * 128, :]
            else:
                src = xs
            dst = big[:, i * 1024:(i + 1) * 1024]
            q = "sp" if S["idx"] == 0 else "act"
            P.add(q, lambda e, dst=dst, src=src: [e.dma_start(out=dst, in_=src)], w=[("big", "load", "st", i)], ndma=1)

    def load_rest(S):
        nt = S["T"] // 128
        for i in range(nt):
            for half in range(2):
                bk = (i % 3) * 2 + half
                def tr(e, i=i, half=half, bk=bk):
                    for k in range(4):
                        c = half * 4 + k
                        ins = e.transpose(banks[bk][:, k * 128:(k + 1) * 128],
                                          big[:, i * 1024 + c * 128:i * 1024 + (c + 1) * 128], ident[:])
                    return ins
                P.add("pe", tr, r=[("big", "load", "st", i), ("ident",)], w=[("ps", bk)])
                dst = xT[:, half * 4:half * 4 + 4, i * 128:(i + 1) * 128]
                srcp = banks[bk][:].rearrange("p (c t) -> p c t", c=4)
                wk = K("x", range(half * 4, half * 4 + 4), i * 128, 128)
                if half == 0:
                    P.add("act", lambda e, dst=dst, srcp=srcp: e.activation(out=dst, in_=srcp, func=AF.Copy),
                          r=[("ps", bk)], w=wk)
                else:
                    P.add("dve", lambda e, dst=dst, srcp=srcp: e.tensor_copy(out=dst, in_=srcp),
                          r=[("ps", bk)], w=wk)
        for bi in range(len(S["blocks"])):
            for dc in range(NCH):
                stats_mm(S, stats_sq(S, dc, bi))
            norm_block(S, bi, (0, "h"))
        if S["Ts"]:
            P.add("sp", lambda e: [e.dma_start(out=stage[0][0:32, 0:512], in_=sc)], w=[("stage", 0)], ndma=1)
            def tr_sc(e):
                for cc in range(4):
                    ins = e.transpose(banks[4][:, cc * 32:(cc + 1) * 32], stage[0][0:32, cc * 128:(cc + 1) * 128],
                                      ident[0:32, 0:32])
                return ins
            P.add("pe", tr_sc, r=[("stage", 0), ("ident",)], w=[("ps", 4)])
            P.add("act", lambda e: e.activation(out=scT[:].rearrange("p c b r -> p c (b r)"),
                                                in_=banks[4][:, 0:128].rearrange("p (c n) -> p c n", c=4), func=AF.Copy),
                  r=[("ps", 4)], w=[("scT",)])

    def sbank(S, bi):
        return 6 if bi == len(S["blocks"]) - 1 else (1, 3)[bi]


    def preswitch():
        P.add("act", lambda e: e.activation(out=misc[:, 120:121], in_=onesm[:, 0:1], func=AF.Ln), r=[("onesm",)], w=[("junk",)])

    def stats_sq(S, dc, bi):
        a, n = S["blocks"][bi]
        nb = len(S["blocks"])
        k = cnt["sq"] % (2 * nb)
        cnt["sq"] += 1
        buf = sqbuf[:, k * n:(k + 1) * n]
        P.add("act", lambda e, dc=dc, a=a, n=n, buf=buf: e.activation(out=buf, in_=xT[:, dc, a:a + n], func=AF.Square),
              r=K("x", dc, a, n), w=[("sqb", k)])
        return (dc, bi, k)

    def stats_mm(S, item):
        dc, bi, k = item
        a, n = S["blocks"][bi]
        buf = sqbuf[:, k * n:(k + 1) * n]
        bank = banks[sbank(S, bi)]
        P.add("pe", lambda e, dc=dc, n=n, buf=buf, bank=bank: e.matmul(bank[:, 0:n], onesm[:], buf, start=(dc == 0),
                                                                     stop=(dc == NCH - 1)),
              r=[("sqb", k), ("onesm",)], w=[("ps", sbank(S, bi))])

    def norm_block(S, bi, nxt, defer=None):
        nidx, out = nxt
        if True:
            a, n = S["blocks"][bi]
            bk = sbank(S, bi)
            while defer:
                defer.pop(0)()
            P.add("act", lambda e, n=n, bk=bk: e.activation(out=rt[:, 0:n], in_=banks[bk][:, 0:n], func=AF.Ln, bias=EPS, scale=1.0),
                  r=[("ps", bk)], w=[("rt",)])
            P.add("act", lambda e, n=n: e.activation(out=rs[:, 0:n], in_=rt[:, 0:n], func=AF.Exp, scale=-0.5),
                  r=[("rt",)], w=[("rs",)])
            thunks = []
            for c in range(NCH):
                if out == "h":
                    dst = hT[:, c, a:a + n]; wk = K("h", c, a, n)
                else:
                    dst = yT[:, c, a:a + n]; wk = [("big", "final", "y", c, t) for t in tiles_of(a, n)]
                def th(dst=dst, c=c, a=a, n=n, wk=wk):
                    P.add("dve", lambda e: e.scalar_tensor_tensor(
                        out=dst, in0=xT[:, c, a:a + n], scalar=cT[:, nidx * 8 + c:nidx * 8 + c + 1], in1=rs[:, 0:n],
                        op0=ALU.mult, op1=ALU.mult),
                        r=K("x", c, a, n) + [("rs",), ("cT",)], w=wk)
                thunks.append(th)
            if defer is None:
                for th in thunks:
                    th()
            else:
                defer.extend(thunks)

    cnt = dict(a=0, b=0, m=0, sq=0, ft=0)

    def ffn(S, layer, which, nxt, prev_tail=None):
        wgv = wg[layer, which].rearrange("(c p) f -> p c f", p=128)
        wuv = wu[layer, which].rearrange("(c p) f -> p c f", p=128)
        wdv = wd[layer, which].rearrange("(f p) d -> p f d", p=128)
        def gu_group(s, f, a, n):
            sg = slot_view(s, 0, 8, 256); su = slot_view(s, 2048, 8, 256)
            fi = f % 2
            k = cnt["a"] % 2
            cnt["a"] += 1
            pg = banks[k]; pu = banks[2 + k]
            def mm(e, sg=sg, su=su, fi=fi, a=a, n=n, pg=pg, pu=pu):
                for c in range(NCH):
                    e.matmul(pg[:, 0:n], sg[:, c, fi * 128:(fi + 1) * 128], hT[:, c, a:a + n],
                             start=(c == 0), stop=(c == NCH - 1))
                for c in range(NCH):
                    ins = e.matmul(pu[:, 0:n], su[:, c, fi * 128:(fi + 1) * 128], hT[:, c, a:a + n],
                                   start=(c == 0), stop=(c == NCH - 1))
                return ins
            P.add("pe", mm, r=K("h", ALLC, a, n) + [("ws", s)], w=[("ps", k), ("ps", 2 + k)])
            P.add("act", lambda e, k=k, n=n, pg=pg: e.activation(out=stmp[k][:, 0:n], in_=pg[:, 0:n], func=AF.Silu),
                  r=[("ps", k)], w=[("stmp", k)])
            P.add("dve", lambda e, k=k, f=f, a=a, n=n, pu=pu: e.tensor_tensor(
                out=aT[:, f, a:a + n], in0=pu[:, 0:n], in1=stmp[k][:, 0:n], op=ALU.mult),
                r=[("ps", 2 + k), ("stmp", k)], w=[("big", "ffn", "a", f, t) for t in tiles_of(a, n)])

        def wpair(f2):
            return witem([(0, (8, 256), wgv[:, :, f2 * 256:(f2 + 1) * 256]),
                          (2048, (8, 256), wuv[:, :, f2 * 256:(f2 + 1) * 256])])

        s0 = wpair(0); s1 = wpair(1)
        for bi_, (a, n) in enumerate(S["blocks"]):
            for f in range(4):
                gu_group(s0 if f < 2 else s1, f, a, n)
                if bi_ == 0 and f == 1 and prev_tail is not None:
                    prev_tail()
        for f2 in range(2, NF // 2):
            s = wpair(f2)
            for (a, n) in S["blocks"]:
                for fi in range(2):
                    gu_group(s, f2 * 2 + fi, a, n)
        if nxt is not None:
            preswitch()
        else:
            final_prefetch()

        def down_group(s, dc, bi):
            a, n = S["blocks"][bi]
            sd = slot_view(s, 0, NF, 128)
            k = 4 + cnt["b"] % 2
            cnt["b"] += 1
            po = banks[k]
            def mm(e, sd=sd, a=a, n=n, po=po):
                for f in range(NF):
                    ins = e.matmul(po[:, 0:n], sd[:, f, :], aT[:, f, a:a + n], start=(f == 0), stop=(f == NF - 1))
                return ins
            P.add("pe", mm, r=[("big", "ffn", "a", f, t) for f in range(NF) for t in tiles_of(a, n)] + [("ws", s)],
                  w=[("ps", k)])
            P.add("dve", lambda e, dc=dc, a=a, n=n, po=po: e.scalar_tensor_tensor(
                out=xT[:, dc, a:a + n], in0=po[:, 0:n], scalar=0.5, in1=xT[:, dc, a:a + n], op0=ALU.mult, op1=ALU.add),
                r=[("ps", k)] + K("x", dc, a, n), w=K("x", dc, a, n))
            return stats_sq(S, dc, bi) if nxt is not None else None

        nb = len(S["blocks"])
        pending = []
        for dc in range(NCH - 2):
            s = witem([(0, (NF, 128), wdv[:, :, dc * 128:(dc + 1) * 128])])
            new = [down_group(s, dc, bi) for bi in range(nb)]
            if nxt is not None:
                for it in pending:
                    stats_mm(S, it)
            pending = new
        s6 = witem([(0, (NF, 128), wdv[:, :, 6 * 128:7 * 128])])
        s7 = witem([(0, (NF, 128), wdv[:, :, 7 * 128:8 * 128])])
        defer = []
        for bi in range(nb):
            i6 = down_group(s6, 6, bi)
            for _ in range(min(4, len(defer))):
                defer.pop(0)()
            if nxt is not None:
                for it in pending:
                    stats_mm(S, it)
                if bi > 0:
                    norm_block(S, bi - 1, nxt, defer)
                    for _ in range(min(5, len(defer))):
                        defer.pop(0)()
            elif bi > 0:
                for i in tiles_of(*S["blocks"][bi - 1]):
                    final_tile(S, i, ((0, 1), (2, 3)))
            i7 = down_group(s7, 7, bi)
            for _ in range(min(4, len(defer))):
                defer.pop(0)()
            pending = [i6, i7]
        while defer:
            defer.pop(0)()

        def tail(pending=pending):
            if nxt is not None:
                for it in pending:
                    stats_mm(S, it)
                norm_block(S, nb - 1, nxt)
        return tail

    def proj_add(S, wv, src, srcname, nxt):
        preswitch()

        def proj_group(s, dc, bi):
            a, n = S["blocks"][bi]
            di = dc % 4
            sw = slot_view(s, (di // 2) * 2048, 8, 256)
            k = 4 + cnt["b"] % 2
            cnt["b"] += 1
            po = banks[k]
            def mm(e, sw=sw, di=di, a=a, n=n, po=po):
                for c in range(NCH):
                    ins = e.matmul(po[:, 0:n], sw[:, c, (di % 2) * 128:(di % 2 + 1) * 128], src[:, c, a:a + n],
                                   start=(c == 0), stop=(c == NCH - 1))
                return ins
            P.add("pe", mm, r=K(srcname, ALLC, a, n) + [("ws", s)], w=[("ps", k)])
            P.add("dve", lambda e, dc=dc, a=a, n=n, po=po: e.tensor_tensor(
                out=xT[:, dc, a:a + n], in0=po[:, 0:n], in1=xT[:, dc, a:a + n], op=ALU.add),
                r=[("ps", k)] + K("x", dc, a, n), w=K("x", dc, a, n))
            return stats_sq(S, dc, bi)

        nb = len(S["blocks"])
        sl = [witem([(0, (8, 256), wv[:, :, d2 * 512:d2 * 512 + 256]),
                     (2048, (8, 256), wv[:, :, d2 * 512 + 256:d2 * 512 + 512])]) for d2 in range(2)]
        fifo = []
        defer = []

        def pop():
            it = fifo.pop(0)
            stats_mm(S, it)
            if it[0] == NCH - 1:
                norm_block(S, it[1], nxt, defer)

        for bi in range(nb):
            for dc in range(NCH):
                fifo.append(proj_group(sl[dc // 4], dc, bi))
                for _ in range(min(2, len(defer))):
                    defer.pop(0)()
                if len(fifo) > 2:
                    pop()
        while defer:
            defer.pop(0)()

        def tail():
            while fifo:
                pop()
            while defer:
                defer.pop(0)()
        return tail

    def transpose_rows_out(srcs, nrows, keys, bank_ids=(7, 5)):
        nchk = len(srcs)
        nb = (nchk + 3) // 4
        for b in range(nb):
            bk = bank_ids[b]
            def tr(e, b=b, bk=bk):
                for k in range(min(4, nchk - 4 * b)):
                    ins = e.transpose(banks[bk][0:nrows, k * 128:(k + 1) * 128], srcs[4 * b + k], ident[:])
                return ins
            P.add("pe", tr, r=list(keys) + [("ident",)], w=[("ps", bk)])
            w_ = min(4, nchk - 4 * b) * 128
            P.add("act", lambda e, b=b, bk=bk, w_=w_: e.activation(out=stage[1][0:nrows, b * 512:b * 512 + w_],
                                                                    in_=banks[bk][0:nrows, 0:w_], func=AF.Copy),
                  r=[("ps", bk)], w=[("stage", 1)])

    def even_mixer(S, prev_tail):
        Tp, Ts, T = S["Tp"], S["Ts"], S["T"]
        nt = T // 128
        R = "even"
        if prev_tail is not None:
            prev_tail()
        s = witem([(0, (8, 512), ewin_v[:, :, 512:1024])])
        sv = slot_view(s, 0, 8, 512)
        for i in range(nt):
            k = cnt["m"] % 2
            cnt["m"] += 1
            pv = banks[k]
            def mm(e, i=i, pv=pv):
                for c in range(NCH):
                    ins = e.matmul(pv[:, :], hT[:, c, i * 128:(i + 1) * 128], sv[:, c, :], start=(c == 0), stop=(c == NCH - 1))
                return ins
            P.add("pe", mm, r=K("h", ALLC, i * 128, 128) + [("ws", s)], w=[("ps", k)])
            P.add("act", lambda e, i=i, pv=pv: e.activation(out=vg_all[:, i, :], in_=pv[:, :], func=AF.Gelu),
                  r=[("ps", k)], w=[("big", R, "vg", i)])
            P.add("dve", lambda e, i=i: e.bn_stats(out=stats[:, i, :], in_=vg_all[:, i, :]),
                  r=[("big", R, "vg", i)], w=[("stats", i)])
            P.add("dve", lambda e, i=i: e.bn_aggr(out=mv[:, i, :], in_=stats[:, i, :]), r=[("stats", i)], w=[("mv", i)])
        P.add("act", lambda e: e.activation(out=rl2[:, 0:nt], in_=mv[:, 0:nt, 1], func=AF.Sqrt, bias=EPS, scale=1.0),
              r=[("mv", i) for i in range(nt)], w=[("rl2",)])
        P.add("dve", lambda e: e.reciprocal(out=rl[:, 0:nt], in_=rl2[:, 0:nt]), r=[("rl2",)], w=[("rl",)])
        nmr = misc[:, 90:99]
        P.add("dve", lambda e: e.scalar_tensor_tensor(out=nmr[:, 0:nt], in0=mv[:, 0:nt, 0], scalar=-1.0, in1=rl[:, 0:nt],
                                                      op0=ALU.mult, op1=ALU.mult),
              r=[("mv", i) for i in range(nt)] + [("rl",)], w=[("nmr",)])
        ln_thunks = []

        def ln_tile(i):
            is_s = (i * 128 >= Tp)
            vg = vg_all[:, i, :]
            P.add("act", lambda e, i=i, vg=vg: e.activation(out=vg, in_=vg, func=AF.Identity, scale=rl[:, i:i + 1],
                                                           bias=nmr[:, i:i + 1]),
                  r=[("big", R, "vg", i), ("nmr",), ("rl",)], w=[("big", R, "vg", i)])
            P.add("dve", lambda e, vg=vg: e.tensor_tensor(out=vg, in0=vg, in1=lng_bc[:], op=ALU.mult),
                  r=[("big", R, "vg", i), ("lnbc",)], w=[("big", R, "vg", i)])
            if not is_s:
                P.add("dve", lambda e, vg=vg, i=i: e.tensor_tensor(out=vbf_all[:, i, :], in0=vg, in1=lnb_bc[:], op=ALU.add),
                      r=[("big", R, "vg", i), ("lnbc",)], w=[("vbf", i)])
            else:
                P.add("dve", lambda e, vg=vg: e.tensor_tensor(out=vfp[:], in0=vg, in1=lnb_bc[:], op=ALU.add),
                      r=[("big", R, "vg", i), ("lnbc",)], w=[("stmp", 0)])
                P.add("dve", lambda e, i=i: e.tensor_copy(out=vbf_all[:, i, :], in_=vfp[:]),
                      r=[("stmp", 0)], w=[("vbf", i)])
                P.add("sp", lambda e: [e.dma_start(out=cvv, in_=vfp[:])], r=[("stmp", 0)], w=[okey("cvv")], ndma=1)

        ln_thunks = [(lambda i=i: ln_tile(i)) for i in range(nt)]
        ln_thunks.pop(0)()
        s = witem([(0, (8, 256), ewin_v[:, :, 0:256]), (2048, (8, 256), ewin_v[:, :, 256:512])])
        for j in range(4):
            sw = slot_view(s, (j // 2) * 2048, 8, 256)
            for (a, n) in S["blocks"]:
                k = 2 + cnt["m"] % 2
                cnt["m"] += 1
                pb = banks[k]
                def mm(e, sw=sw, j=j, a=a, n=n, pb=pb):
                    for c in range(NCH):
                        ins = e.matmul(pb[:, 0:n], sw[:, c, (j % 2) * 128:(j % 2 + 1) * 128], hT[:, c, a:a + n],
                                       start=(c == 0), stop=(c == NCH - 1))
                    return ins
                P.add("pe", mm, r=K("h", ALLC, a, n) + [("ws", s)], w=[("ps", k)])
                P.add("act", lambda e, j=j, a=a, n=n, pb=pb: e.activation(out=uT[:, j, a:a + n], in_=pb[:, 0:n], func=AF.Gelu),
                      r=[("ps", k)], w=[("big", R, "u", j, t) for t in tiles_of(a, n)])
                if ln_thunks:
                    ln_thunks.pop(0)()
        while ln_thunks:
            ln_thunks.pop(0)()
        for i in range(nt):
            is_s = (i * 128 >= Tp)
            WT = WTs if is_s else WTp
            bsb = bsb_s if is_s else bsb_p
            wkey = ("WTs",) if is_s else ("WTp",)
            bkey = ("bs_s",) if is_s else ("bs_p",)
            k = cnt["m"] % 2
            cnt["m"] += 1
            py = banks[k]
            def mm(e, i=i, WT=WT, py=py):
                for hh in range(4):
                    ins = e.matmul(py[:, hh * 128:(hh + 1) * 128], vbf_all[:, i, hh * 128:(hh + 1) * 128], WT[:, hh, :],
                                   start=True, stop=True)
                return ins
            P.add("pe", mm, r=[("vbf", i), wkey], w=[("ps", k)])
            tb_ = rt if i % 2 == 0 else rs
            tk = ("rt",) if i % 2 == 0 else ("rs",)
            P.add("dve", lambda e, py=py, tb_=tb_, bsb=bsb: e.tensor_tensor(out=tb_[:], in0=py[:], in1=bsb[:], op=ALU.add),
                  r=[("ps", k), bkey], w=[tk])
            P.add("pool", lambda e, i=i, tb_=tb_: e.tensor_tensor(
                out=mixT[:, 0:4, i * 128:(i + 1) * 128], in0=tb_[:].rearrange("p (h t) -> p h t", h=4),
                in1=uT[:, 0:4, i * 128:(i + 1) * 128], op=ALU.mult),
                r=[tk] + [("big", R, "u", j, i) for j in range(4)], w=K("mix", range(4), i * 128, 128))
        for pp in range(2):
            s1 = witem([(0, (8, 256), ewin_v[:, :, 1536 + pp * 256:1536 + (pp + 1) * 256]),
                        (2048, (8, 256), ewin_v[:, :, 2048 + pp * 256:2048 + (pp + 1) * 256])])
            s2 = witem([(0, (8, 256), ewin_v[:, :, 1024 + pp * 256:1024 + (pp + 1) * 256])])
            sc_ = slot_view(s1, 0, 8, 256); sx_ = slot_view(s1, 2048, 8, 256); sb_ = slot_view(s2, 0, 8, 256)
            for ci in range(2):
                cc = pp * 2 + ci
                csl = slice(ci * 128, (ci + 1) * 128)
                w0 = cT[:, 56 + cc:57 + cc]; w1 = cT[:, 60 + cc:61 + cc]; w2 = cT[:, 64 + cc:65 + cc]
                extk = [("big", R, "ext0")] + [("big", R, "ext", t) for t in range(Tp // 128)]
                if S["idx"] == 0:
                    P.add("dve", lambda e: e.memset(ext[:, 0:2], 0.0), w=[("big", R, "ext0")])
                else:
                    P.add("dve", lambda e, cc=cc: e.tensor_copy(out=ext[:, 0:2], in_=convst[:, cc, :]),
                          r=[("convst", cc)], w=[("big", R, "ext0")])
                    P.add("dve", lambda e, cc=cc: e.tensor_copy(out=ext_s[:, :, 0:2], in_=scT[:, cc, :, :]),
                          r=[("scT",)], w=[("big", R, "exts0")])
                for (a, n) in S["blocks"]:
                    pn = max(0, min(a + n, Tp) - a)
                    sn = n - pn
                    k = cnt["m"] % 2
                    cnt["m"] += 1
                    pc = banks[k]; px = banks[2 + k]
                    def mm(e, a=a, n=n, pc=pc, px=px, csl=csl, sc_=sc_, sx_=sx_):
                        for c in range(NCH):
                            e.matmul(pc[:, 0:n], sc_[:, c, csl], hT[:, c, a:a + n], start=(c == 0), stop=(c == NCH - 1))
                        for c in range(NCH):
                            ins = e.matmul(px[:, 0:n], sx_[:, c, csl], hT[:, c, a:a + n], start=(c == 0), stop=(c == NCH - 1))
                        return ins
                    P.add("pe", mm, r=K("h", ALLC, a, n) + [("ws", s1)], w=[("ps", k), ("ps", 2 + k)])
                    P.add("act", lambda e, a=a, n=n, px=px: e.activation(out=xin_s[:, a:a + n], in_=px[:, 0:n], func=AF.Copy),
                          r=[("ps", 2 + k)], w=[("big", R, "xin", t) for t in tiles_of(a, n)])
                    if pn:
                        P.add("dve", lambda e, a=a, pn=pn, pc=pc: e.tensor_tensor(
                            out=ext[:, 2 + a:2 + a + pn], in0=pc[:, 0:pn], in1=xin_s[:, a:a + pn], op=ALU.mult),
                            r=[("ps", k)] + [("big", R, "xin", t) for t in tiles_of(a, pn)],
                            w=[("big", R, "ext", t) for t in tiles_of(a, pn)])
                    if sn:
                        P.add("dve", lambda e, a=a, pn=pn, sn=sn, pc=pc: e.tensor_tensor(
                            out=ext_s[:, :, 2:10], in0=pc[:, pn:pn + sn].rearrange("p (b k) -> p b k", b=16),
                            in1=xin_s[:, a + pn:a + pn + sn].rearrange("p (b k) -> p b k", b=16), op=ALU.mult),
                            r=[("ps", k)] + [("big", R, "xin", t) for t in tiles_of(a + pn, sn)],
                            w=[("big", R, "exts")])
                    k2 = 4 + cnt["b"] % 2
                    cnt["b"] += 1
                    pb = banks[k2]
                    def mm2(e, a=a, n=n, pb=pb, csl=csl, sb_=sb_):
                        for c in range(NCH):
                            ins = e.matmul(pb[:, 0:n], sb_[:, c, csl], hT[:, c, a:a + n], start=(c == 0), stop=(c == NCH - 1))
                        return ins
                    P.add("pe", mm2, r=K("h", ALLC, a, n) + [("ws", s2)], w=[("ps", k2)])
                    P.add("act", lambda e, a=a, n=n, pb=pb: e.activation(out=gb_s[:, a:a + n], in_=pb[:, 0:n], func=AF.Copy),
                          r=[("ps", k2)], w=[("big", R, "gb", t) for t in tiles_of(a, n)])
                    if pn:
                        ek = [("big", R, "ext0")] + [("big", R, "ext", t) for t in range((a + pn) // 128)]
                        ak = ("big", R, "acc", a)
                        P.add("dve", lambda e, w0=w0, a=a, pn=pn: e.tensor_scalar(
                            out=acc[:, a:a + pn], in0=ext[:, a:a + pn], scalar1=w0, scalar2=None, op0=ALU.mult),
                            r=ek + [("cT",)], w=[ak])
                        P.add("dve", lambda e, w1=w1, a=a, pn=pn: e.scalar_tensor_tensor(
                            out=acc[:, a:a + pn], in0=ext[:, a + 1:a + pn + 1], scalar=w1, in1=acc[:, a:a + pn],
                            op0=ALU.mult, op1=ALU.add), r=ek + [ak], w=[ak])
                        P.add("dve", lambda e, w2=w2, a=a, pn=pn: e.scalar_tensor_tensor(
                            out=acc[:, a:a + pn], in0=ext[:, a + 2:a + pn + 2], scalar=w2, in1=acc[:, a:a + pn],
                            op0=ALU.mult, op1=ALU.add), r=ek + [ak], w=[ak])
                        P.add("dve", lambda e, cc=cc, a=a, pn=pn: e.tensor_tensor(
                            out=mixT[:, 4 + cc, a:a + pn], in0=acc[:, a:a + pn], in1=gb_s[:, a:a + pn], op=ALU.mult),
                            r=[ak] + [("big", R, "gb", t) for t in tiles_of(a, pn)], w=K("mix", 4 + cc, a, pn))
                P.add("dve", lambda e, cc=cc: e.tensor_copy(out=convst[:, cc, :], in_=ext[:, Tp:Tp + 2]),
                      r=extk, w=[("convst", cc)])
                if Ts:
                    accs = acc[:, Tp:Tp + 128].rearrange("p (b k) -> p b k", b=16)
                    gbs = gb_s[:, Tp:Tp + 128].rearrange("p (b k) -> p b k", b=16)
                    exk = [("big", R, "exts0"), ("big", R, "exts")]
                    P.add("dve", lambda e, w0=w0, accs=accs: e.tensor_scalar(out=accs, in0=ext_s[:, :, 0:8], scalar1=w0,
                                                                            scalar2=None, op0=ALU.mult),
                          r=exk + [("cT",)], w=[("big", R, "accs")])
                    P.add("dve", lambda e, w1=w1, accs=accs: e.scalar_tensor_tensor(
                        out=accs, in0=ext_s[:, :, 1:9], scalar=w1, in1=accs, op0=ALU.mult, op1=ALU.add),
                        r=exk + [("big", R, "accs")], w=[("big", R, "accs")])
                    P.add("dve", lambda e, w2=w2, accs=accs: e.scalar_tensor_tensor(
                        out=accs, in0=ext_s[:, :, 2:10], scalar=w2, in1=accs, op0=ALU.mult, op1=ALU.add),
                        r=exk + [("big", R, "accs")], w=[("big", R, "accs")])
                    P.add("dve", lambda e, cc=cc, accs=accs, gbs=gbs: e.tensor_tensor(
                        out=mixT[:, 4 + cc, Tp:Tp + 128].rearrange("p (b k) -> p b k", b=16), in0=accs, in1=gbs, op=ALU.mult),
                        r=[("big", R, "accs"), ("big", R, "gb", Tp // 128)], w=K("mix", 4 + cc, Tp, 128))
                    P.add("dve", lambda e, cc=cc: e.tensor_copy(out=cvs_buf[:, cc, :, :], in_=ext_s[:, :, 8:10]),
                          r=exk, w=[("cvs_buf", cc)])
        tail_ = proj_add(S, ewout_v, mixT, "mix", (2, "h"))
        if S["idx"] == 1:
            transpose_rows_out([convst[:, c, :] for c in range(4)], 2, [("convst", c) for c in range(4)])
            P.add("sp", lambda e: [e.dma_start(out=cvp, in_=stage[1][0:2, 0:512])], r=[("stage", 1)], w=[okey("cvp")], ndma=1)
            transpose_rows_out([cvs_buf[:, c, :, :].rearrange("p b r -> p (b r)") for c in range(4)], 32,
                               [("cvs_buf", c) for c in range(4)])
            P.add("sp", lambda e: [e.dma_start(out=cvs, in_=stage[1][0:32, 0:512])], r=[("stage", 1)], w=[okey("cvs")], ndma=1)
        return tail_

    def odd_prefetch(S):
        for hb in range(2):
            P.add("sp", lambda e, hb=hb: [e.dma_start(out=stage[hb][0:120, :], in_=spl[hb * 120:(hb + 1) * 120, :])],
                  w=[("stage", hb)], ndma=1)

    def odd_mixer(S, prev_tail):
        Tp, Ts, T = S["Tp"], S["Ts"], S["T"]
        R = "odd"
        L = 15 + Tp
        if prev_tail is not None:
            prev_tail()
        if S["idx"] == 0:
            P.add("dve", lambda e: e.memset(p_ext[:, :, 0:15], 0.0), w=[("big", R, "p0")])
        else:
            P.add("dve", lambda e: e.tensor_copy(out=p_ext[:, :, 0:15], in_=poolst[:]),
                  r=[("poolst", c) for c in range(NCH)], w=[("big", R, "p0")])
            for hb in range(2):
                for half in range(2):
                    bk = (7, 5)[half]
                    def tr(e, half=half, bk=bk, hb=hb):
                        for k in range(4):
                            c = half * 4 + k
                            ins = e.transpose(banks[bk][:, k * 128:k * 128 + 120], stage[hb][0:120, c * 128:(c + 1) * 128],
                                              ident[0:120, 0:120])
                        return ins
                    P.add("pe", tr, r=[("stage", hb), ("ident",)], w=[("ps", bk)])
                    for k in range(4):
                        c = half * 4 + k
                        P.add("act", lambda e, c=c, k=k, bk=bk, hb=hb: e.activation(
                            out=ps_ext[:, c, hb * 8:(hb + 1) * 8, 0:15],
                            in_=banks[bk][:, k * 128:k * 128 + 120].rearrange("p (b r) -> p b r", b=8), func=AF.Copy),
                            r=[("ps", bk)], w=[("big", R, "ps0", c, hb)])
            P.add("sp", lambda e: [e.dma_start(out=pls[:, 0:7, :], in_=spl.rearrange("(b r) d -> b r d", r=15)[:, 8:15, :])],
                  w=[okey("pls_a")], ndma=1)
        for d2 in (1, 0):
            s = witem([(0, (8, 256), owin_v[:, :, d2 * 512:d2 * 512 + 256]),
                       (2048, (8, 256), owin_v[:, :, d2 * 512 + 256:d2 * 512 + 512])])
            for di in (2, 3, 0, 1):
                dc = d2 * 4 + di
                sw = slot_view(s, (di // 2) * 2048, 8, 256)
                for (a, n) in S["blocks"]:
                    pn = max(0, min(a + n, Tp) - a)
                    sn = n - pn
                    k = cnt["m"] % 2
                    cnt["m"] += 1
                    pb = banks[k]
                    def mm(e, sw=sw, di=di, a=a, n=n, pb=pb):
                        for c in range(NCH):
                            ins = e.matmul(pb[:, 0:n], sw[:, c, (di % 2) * 128:(di % 2 + 1) * 128], hT[:, c, a:a + n],
                                           start=(c == 0), stop=(c == NCH - 1))
                        return ins
                    P.add("pe", mm, r=K("h", ALLC, a, n) + [("ws", s)], w=[("ps", k)])
                    if pn:
                        P.add("act", lambda e, dc=dc, a=a, pn=pn, pb=pb: e.activation(
                            out=p_ext[:, dc, 15 + a:15 + a + pn], in_=pb[:, 0:pn], func=AF.Copy),
                            r=[("ps", k)], w=[("big", R, "p", dc, t) for t in tiles_of(a, pn)])
                    if sn:
                        P.add("act", lambda e, dc=dc, pn=pn, sn=sn, pb=pb: e.activation(
                            out=ps_ext[:, dc, :, 15:23], in_=pb[:, pn:pn + sn].rearrange("p (b k) -> p b k", b=16), func=AF.Copy),
                            r=[("ps", k)], w=[("big", R, "psn", dc)])
                g = dc // 2
                w_ = 2 << g
                E0 = p_ext[:, dc, :]
                pk = [("big", R, "p0")] + [("big", R, "p", dc, t) for t in range(Tp // 128)]
                chain = [(t1, E0, 1), (t2, t1, 2), (t1, t2, 4), (t2, t1, 8)][:g + 1]
                tk = {id(t1): ("big", R, "t1"), id(t2): ("big", R, "t2")}
                for (dst, src, sh) in chain:
                    lo = 2 * sh - 1
                    rk = pk if src is E0 else [tk[id(src)]]
                    P.add("dve", lambda e, dst=dst, src=src, sh=sh, lo=lo: e.tensor_tensor(
                        out=dst[:, lo:L], in0=src[:, lo:L], in1=src[:, lo - sh:L - sh], op=ALU.add),
                        r=rk, w=[tk[id(dst)]])
                Sw = chain[-1][0]
                P.add("dve", lambda e, dc=dc, Sw=Sw, w_=w_, E0=E0: e.scalar_tensor_tensor(
                    out=mixT[:, dc, 0:Tp], in0=Sw[:, 15:15 + Tp], scalar=1.0 / w_, in1=E0[:, 15:15 + Tp],
                    op0=ALU.mult, op1=ALU.subtract),
                    r=[tk[id(Sw)]] + pk, w=K("mix", dc, 0, Tp))
                if S["idx"] == 0:
                    P.add("dve", lambda e, Sw=Sw, g=g: e.tensor_tensor(out=tmp16[:], in0=Sw[:, 15:31], in1=RC[:, g, :], op=ALU.mult),
                          r=[tk[id(Sw)], ("RC",)], w=[("tmp16",)])
                    P.add("dve", lambda e, dc=dc, E0=E0: e.tensor_tensor(out=mixT[:, dc, 0:16], in0=tmp16[:], in1=E0[:, 15:31],
                                                                        op=ALU.subtract),
                          r=[("tmp16",)] + pk, w=K("mix", dc, 0, 16))
                P.add("act", lambda e, dc=dc, E0=E0: e.activation(out=poolst[:, dc, :], in_=E0[:, Tp:Tp + 15], func=AF.Copy),
                      r=pk, w=[("poolst", dc)])
                if Ts:
                    Es = ps_ext[:, dc, :, :]
                    sk = [("big", R, "ps0", dc, 0), ("big", R, "ps0", dc, 1), ("big", R, "psn", dc)]
                    chs = [(ts1, Es, 1), (ts2, ts1, 2), (ts1, ts2, 4), (ts2, ts1, 8)][:g + 1]
                    tks = {id(ts1): ("big", R, "ts1"), id(ts2): ("big", R, "ts2")}
                    for (dst, src, sh) in chs:
                        lo = 2 * sh - 1
                        rk = sk if src is Es else [tks[id(src)]]
                        P.add("dve", lambda e, dst=dst, src=src, sh=sh, lo=lo: e.tensor_tensor(
                            out=dst[:, :, lo:23], in0=src[:, :, lo:23], in1=src[:, :, lo - sh:23 - sh], op=ALU.add),
                            r=rk, w=[tks[id(dst)]])
                    Ss = chs[-1][0]
                    P.add("dve", lambda e, dc=dc, Ss=Ss, w_=w_, Es=Es: e.scalar_tensor_tensor(
                        out=mixT[:, dc, Tp:Tp + 128].rearrange("p (b k) -> p b k", b=16), in0=Ss[:, :, 15:23],
                        scalar=1.0 / w_, in1=Es[:, :, 15:23], op0=ALU.mult, op1=ALU.subtract),
                        r=[tks[id(Ss)]] + sk, w=K("mix", dc, Tp, 128))
        if S["idx"] == 1:
            transpose_rows_out([poolst[:, c, :] for c in range(NCH)], 15, [("poolst", c) for c in range(NCH)])
            P.add("sp", lambda e: [e.dma_start(out=plp, in_=stage[1][0:15, :])], r=[("stage", 1)], w=[okey("plp")], ndma=1)
            P.add("act", lambda e: e.activation(out=stage[0][:].rearrange("p (c b k) -> p c b k", c=8, b=16),
                                                in_=ps_ext[:, :, :, 15:23], func=AF.Copy),
                  r=[("big", R, "psn", c) for c in range(NCH)], w=[("stage", 0)])
            transpose_rows_out([stage[0][:, c * 128:(c + 1) * 128] for c in range(NCH)], 128, [("stage", 0)])
            P.add("sp", lambda e: [e.dma_start(out=pls[:, 7:15, :], in_=stage[1][:, :])], r=[("stage", 1)],
                  w=[okey("pls_b")], ndma=1)
        s = witem([(0, (8, 256), poolw_v)])
        spw = slot_view(s, 0, 8, 256)
        for oc in (6, 7, 4, 5, 2, 3, 0, 1):
            g = oc // 2
            osl = slice((oc % 2) * 128, (oc % 2 + 1) * 128)
            for (a, n) in S["blocks"]:
                k = 2 + cnt["m"] % 2
                cnt["m"] += 1
                pb = banks[k]
                def mm(e, g=g, osl=osl, a=a, n=n, pb=pb):
                    for kk in range(2):
                        ins = e.matmul(pb[:, 0:n], spw[:, 2 * g + kk, osl], mixT[:, 2 * g + kk, a:a + n],
                                       start=(kk == 0), stop=(kk == 1))
                    return ins
                P.add("pe", mm, r=K("mix", (2 * g, 2 * g + 1), a, n) + [("ws", s)], w=[("ps", k)])
                P.add("act", lambda e, oc=oc, a=a, n=n, pb=pb: e.activation(
                    out=hT[:, oc, a:a + n], in_=pb[:, 0:n], func=AF.Identity, scale=cT[:, 68 + oc:69 + oc]),
                    r=[("ps", k), ("cT",)], w=K("h", oc, a, n))
        return proj_add(S, owout_v, hT, "h", (5, "h"))

    def final_prefetch():
        fgv = fg.rearrange("(o c) p -> o (c p)", o=1)
        P.add("sp", lambda e: [e.dma_start(out=stmp[0][:].unsqueeze(1), in_=fgv[:, 0:512].partition_broadcast(128)),
                               e.dma_start(out=stmp[1][:].unsqueeze(1), in_=fgv[:, 512:1024].partition_broadcast(128))],
              w=[("stmp", 0), ("stmp", 1)], ndma=2)

    _fb = {}

    def final_bufs():
        if "b" not in _fb:
            hflat = hT[:].rearrange("p c t -> p (c t)").bitcast(F32)
            mflat = mixT[:].rearrange("p c t -> p (c t)").bitcast(F32)
            sb_ = [(stage[0][:], [("stage", 0)]), (stage[1][:], [("stage", 1)])]
            for j in range(4):
                sb_.append((hflat[:, j * 1024:(j + 1) * 1024], [("h", b // 9, b % 9) for b in range(16 * j, 16 * j + 16)]))
            for j in range(4):
                sb_.append((mflat[:, j * 1024:(j + 1) * 1024], [("mix", b // 9, b % 9) for b in range(16 * j, 16 * j + 16)]))
            _fb["b"] = sb_
        return _fb["b"]

    def final_tile(S, i, bank_pairs):
        sbufs = final_bufs()
        st, stk = sbufs[i % len(sbufs)]
        bk0 = bank_pairs[cnt["ft"] % len(bank_pairs)]
        cnt["ft"] += 1
        for half in range(2):
            bk = bk0[half]
            def tr(e, i=i, half=half, bk=bk):
                for k in range(4):
                    c = half * 4 + k
                    ins = e.transpose(banks[bk][:, k * 128:(k + 1) * 128], xT[:, c, i * 128:(i + 1) * 128], ident[:])
                return ins
            P.add("pe", tr, r=K("x", range(half * 4, half * 4 + 4), i * 128, 128) + [("ident",)], w=[("ps", bk)])
            junk = rt if half == 0 else rs
            jk = ("rt",) if half == 0 else ("rs",)
            P.add("act", lambda e, i=i, half=half, bk=bk, junk=junk: e.activation(
                out=junk[:], in_=banks[bk][:], func=AF.Square, accum_out=ssq[:, 2 * i + half:2 * i + half + 1]),
                r=[("ps", bk)], w=[jk, ("ssq", i, half)])
        P.add("act", lambda e, i=i: e.activation(out=fsum[:, i:i + 1], in_=ssq[:, 2 * i:2 * i + 1], func=AF.Identity,
                                                bias=EPS, scale=1.0 / D),
              r=[("ssq", i, 0)], w=[("fsum", i)])
        P.add("act", lambda e, i=i: e.activation(out=fsq[:, i:i + 1], in_=ssq[:, 2 * i + 1:2 * i + 2], func=AF.Sqrt,
                                                bias=fsum[:, i:i + 1], scale=1.0 / D),
              r=[("fsum", i), ("ssq", i, 1)], w=[("fsq", i)])
        P.add("dve", lambda e, i=i: e.reciprocal(out=frs[:, i:i + 1], in_=fsq[:, i:i + 1]), r=[("fsq", i)], w=[("frs", i)])
        for half in range(2):
            bk = bk0[half]
            P.add("dve", lambda e, i=i, half=half, bk=bk, st=st: e.scalar_tensor_tensor(
                out=st[:, half * 512:(half + 1) * 512], in0=banks[bk][:], scalar=frs[:, i:i + 1], in1=stmp[half][:],
                op0=ALU.mult, op1=ALU.mult),
                r=[("ps", bk), ("frs", i), ("stmp", half)], w=list(stk))
        if i * 128 < S["Tp"]:
            dst = yp[S["poff"] + i * 128:S["poff"] + (i + 1) * 128, :]
        else:
            dst = ys
        P.add("sp", lambda e, st=st, dst=dst: [e.dma_start(out=dst, in_=st)], r=list(stk),
              w=[okey("y")], ndma=1)

    def final_out(S):
        a, n = S["blocks"][-1]
        for i in tiles_of(a, n):
            final_tile(S, i, ((0, 1), (2, 3), (4, 5)))

    P.dry = True
    emit_all()
    P.dry = False
    del out_keys[:]
    cnt.update(a=0, b=0, m=0, sq=0, ft=0)
    emit_all()

    sem_names = {"pe": "s_pe", "act": "s_act", "dve": "s_dve", "pool": "s_pool", "sp": "s_sp"}
    sems = {k: es.enter_context(nc.semaphore(v)) for k, v in sem_names.items()}
    rings = {"sp": [es.enter_context(nc.semaphore(f"r_sp{i}")) for i in range(8)],
             "pool": [es.enter_context(nc.semaphore(f"r_pool{i}")) for i in range(8)],
             "act": [es.enter_context(nc.semaphore(f"r_act{i}")) for i in range(10)]}
    block = es.enter_context(nc.Block())
    P.emit(nc, block, sems, rings)
    _DBG["P"] = P
    es.close()
    return nc


_NC_CACHE = {}


def kernel(x_prompt, x_sample, state_conv, state_pool, norm_g, final_norm_g,
           ffn_w_gate, ffn_w_up, ffn_w_down, e_w_in, e_ln_g, e_ln_b, e_sgu_w, e_sgu_b,
           e_conv_w, e_w_out, o_w_in, o_pool_w, o_pool_scale, o_w_out):
    f = lambda a: np.ascontiguousarray(np.asarray(a, dtype=np.float32))
    if "nc" not in _NC_CACHE:
        _NC_CACHE["nc"] = build_program()
    nc = _NC_CACHE["nc"]
    x_prompt = f(x_prompt); x_sample = f(x_sample); state_conv = f(state_conv); state_pool = f(state_pool)
    shared = dict(
        ng=f(norm_g).reshape(48, 128), fg=f(final_norm_g).reshape(8, 128),
        wg=f(ffn_w_gate), wu=f(ffn_w_up), wd=f(ffn_w_down),
        ewin=f(e_w_in)[0], lng=f(e_ln_g).reshape(1, 512), lnb=f(e_ln_b).reshape(1, 512),
        sguw=f(e_sgu_w)[0], sgub=f(e_sgu_b)[0], convw=f(e_conv_w).reshape(12, 128), ewout=f(e_w_out)[0],
        owin=f(o_w_in)[0], poolw=f(o_pool_w)[0], pscale=f(o_pool_scale).reshape(8, 128), owout=f(o_w_out)[0],
    )
    in_maps = []
    for i in range(NCORES):
        m = dict(shared)
        m["xp"] = x_prompt[i]
        m["xs"] = x_sample[16 * i:16 * (i + 1)].reshape(128, D)
        m["sc"] = state_conv[0, 16 * i:16 * (i + 1)].reshape(32, 512)
        m["spl"] = state_pool[0, 16 * i:16 * (i + 1)].reshape(240, D)
        in_maps.append(m)
    res = run_bass_kernel_spmd(nc, in_maps, core_ids=list(range(NCORES)))
    R = res.results
    y_prompt = np.stack([R[i]["yp"] for i in range(NCORES)], 0).astype(np.float32)
    y_sample = np.concatenate([R[i]["ys"].reshape(16, 8, D) for i in range(NCORES)], 0).astype(np.float32)
    conv_prompt = np.stack([R[i]["cvp"] for i in range(NCORES)], 0)[None].astype(np.float32)
    conv_sample = np.concatenate([R[i]["cvs"].reshape(16, 2, 512) for i in range(NCORES)], 0)[None].astype(np.float32)
    chunk_v = np.concatenate([R[i]["cvv"].reshape(16, 8, 512) for i in range(NCORES)], 0)[None].astype(np.float32)
    pool_prompt = np.stack([R[i]["plp"] for i in range(NCORES)], 0)[None].astype(np.float32)
    pool_sample = np.concatenate([R[i]["pls"] for i in range(NCORES)], 0)[None].astype(np.float32)
    return (y_prompt, y_sample, conv_prompt, conv_sample, chunk_v, pool_prompt, pool_sample)
```

```python
import numpy as np
import concourse.bass as bass
import concourse.mybir as mybir
from concourse.bass_utils import run_bass_kernel_spmd

F32 = mybir.dt.float32
BF16 = mybir.dt.bfloat16
AF = mybir.ActivationFunctionType
ALU = mybir.AluOpType

NCORES = 8
_DBG = {}
D = 1024
NCH = 8
FF = 2816
NF = 22
SEQ = 2048
EPS = 1e-6
TMAX = 1152
NBIG = 14080
NSLOT = 4
SLOT_E = 4096
SAME_ENG_SYNC = True

SBS = [
    dict(idx=0, Tp=1024, Ts=0, T=1024, poff=0, blocks=[(0, 512), (512, 512)]),
    dict(idx=1, Tp=1024, Ts=128, T=1152, poff=1024, blocks=[(0, 384), (384, 384), (768, 384)]),
]


class Op:
    __slots__ = ("eng", "fn", "deps", "ndma", "sem", "val", "needed", "prev_val")

    def __init__(self, eng, fn, deps, ndma):
        self.eng, self.fn, self.deps, self.ndma = eng, fn, deps, ndma
        self.sem = None
        self.val = 0
        self.prev_val = 0
        self.needed = False


class Prog:
    ENGS = ("pe", "act", "dve", "pool", "sp")

    def __init__(self):
        self.ops = []
        self.last_w = {}
        self.readers = {}
        self.dry = False
        self.witems = []
        self.big_role = None
        self.big_cur = {}
        self.big_prev = {}

    def add(self, eng, fn, r=(), w=(), ndma=0):
        if self.dry:
            return -1
        idx = len(self.ops)
        deps = set()
        touches_big = False
        for k in r:
            if k[0] == "big":
                touches_big = True
                self._big_role(k[1])
            d = self.last_w.get(k)
            if d is not None:
                deps.add(d)
        for k in w:
            if k[0] == "big":
                touches_big = True
                self._big_role(k[1])
            d = self.last_w.get(k)
            if d is not None:
                deps.add(d)
            deps.update(self.readers.get(k, ()))
        if touches_big:
            deps.update(self.big_prev.values())
            self.big_cur[eng] = idx
        for k in r:
            self.readers.setdefault(k, []).append(idx)
        for k in w:
            self.last_w[k] = idx
            self.readers[k] = []
        deps.discard(idx)
        self.ops.append(Op(eng, fn, deps, ndma))
        return idx

    def _big_role(self, role):
        if role != self.big_role:
            for e, i in self.big_cur.items():
                self.big_prev[e] = max(self.big_prev.get(e, -1), i)
            self.big_cur = {}
            self.big_role = role

    def emit(self, nc, block, sems, rings):
        ops = self.ops
        for op in ops:
            for d in op.deps:
                ops[d].needed = True
        cnt = {e: 0 for e in self.ENGS}
        ring_tot = {q: [0] * len(rings[q]) for q in rings}
        ring_n = {q: 0 for q in rings}
        for op in ops:
            if op.ndma:
                q = op.eng
                j = ring_n[q] % len(rings[q])
                ring_n[q] += 1
                op.sem = rings[q][j]
                op.prev_val = ring_tot[q][j]
                ring_tot[q][j] += 16 * op.ndma
                op.val = ring_tot[q][j]
            elif op.needed:
                cnt[op.eng] += 1
                op.sem = sems[op.eng]
                op.val = cnt[op.eng]

        def run_engine(ename, e):
            waited = {}
            for op in ops:
                if op.eng != ename:
                    continue
                want = {}
                for d in op.deps:
                    dop = ops[d]
                    if dop.ndma == 0 and dop.eng == ename:
                        if ename == "pe" or not SAME_ENG_SYNC:
                            continue
                    key = id(dop.sem)
                    if key not in want or want[key][1] < dop.val:
                        want[key] = (dop.sem, dop.val)
                if op.ndma and op.prev_val > 0:
                    key = id(op.sem)
                    if key not in want or want[key][1] < op.prev_val:
                        want[key] = (op.sem, op.prev_val)
                for key, (sem, val) in want.items():
                    if waited.get(key, 0) >= val:
                        continue
                    e.wait_ge(sem, val)
                    waited[key] = val
                if op.fn is None:
                    continue
                ins = op.fn(e)
                if op.ndma:
                    assert len(ins) == op.ndma
                    for i_ in ins:
                        i_.then_inc(op.sem, 16)
                elif op.needed:
                    ins.then_inc(op.sem, 1)

        @block.tensor
        def _(e):
            run_engine("pe", e)

        @block.scalar
        def _(e):
            run_engine("act", e)

        @block.vector
        def _(e):
            run_engine("dve", e)

        @block.gpsimd
        def _(e):
            run_engine("pool", e)

        @block.sync
        def _(e):
            run_engine("sp", e)


def tiles_of(a, n):
    return range(a // 128, (a + n + 127) // 128)


def K(name, cs, a, n):
    if isinstance(cs, int):
        cs = (cs,)
    return [(name, c, t) for c in cs for t in tiles_of(a, n)]


ALLC = tuple(range(NCH))


def build_program():
    nc = bass.Bass("TRN2", target_bir_lowering=False)

    def din(name, shape):
        return nc.dram_tensor(name, list(shape), F32, kind="ExternalInput").ap()

    def dout(name, shape):
        return nc.dram_tensor(name, list(shape), F32, kind="ExternalOutput").ap()

    xp = din("xp", [SEQ, D]); xs = din("xs", [128, D])
    sc = din("sc", [32, 512]); spl = din("spl", [240, D])
    ng = din("ng", [48, 128]); fg = din("fg", [8, 128])
    wg = din("wg", [2, 2, D, FF]); wu = din("wu", [2, 2, D, FF]); wd = din("wd", [2, 2, FF, D])
    ewin = din("ewin", [D, 2560]); lng = din("lng", [1, 512]); lnb = din("lnb", [1, 512])
    sguw = din("sguw", [4, 128, 128]); sgub = din("sgub", [4, 128])
    convw = din("convw", [12, 128]); ewout = din("ewout", [D, D])
    owin = din("owin", [D, D]); poolw = din("poolw", [4, 256, 256]); pscale = din("pscale", [8, 128])
    owout = din("owout", [D, D])

    yp = dout("yp", [SEQ, D]); ys = dout("ys", [128, D])
    cvp = dout("cvp", [2, 512]); cvs = dout("cvs", [32, 512]); cvv = dout("cvv", [128, 512])
    plp = dout("plp", [15, D]); pls = dout("pls", [16, 15, D])

    ewin_v = ewin.rearrange("(c p) j -> p c j", p=128)
    ewout_v = ewout.rearrange("(c p) j -> p c j", p=128)
    owin_v = owin.rearrange("(c p) j -> p c j", p=128)
    owout_v = owout.rearrange("(c p) j -> p c j", p=128)
    poolw_v = poolw.rearrange("g (k p) o -> p (g k) o", p=128)

    from contextlib import ExitStack
    es = ExitStack()

    def sb(name, shape, dt=F32):
        return es.enter_context(nc.sbuf_tensor(name, list(shape), dt))

    xT = sb("xT", [128, NCH, TMAX])
    hT = sb("hT", [128, NCH, TMAX], BF16)
    mixT = sb("mixT", [128, NCH, TMAX], BF16)
    big = sb("big", [128, NBIG])
    wslots = [sb(f"wslot{i}", [128, SLOT_E], BF16) for i in range(NSLOT)]
    stage = [sb(f"stage{i}", [128, 1024]) for i in range(2)]
    stmp = [sb(f"stmp{i}", [128, 512]) for i in range(2)]
    rt = sb("rt", [128, 512]); rs = sb("rs", [128, 512])
    lng_bc = sb("lng_bc", [128, 512]); lnb_bc = sb("lnb_bc", [128, 512])
    vbf_all = sb("vbf_all", [128, 9, 512], BF16)
    vfp = stmp[0]
    sqbuf = sb("sqbuf", [128, 2304], BF16)
    bsb_p = sb("bsb_p", [128, 512]); bsb_s = sb("bsb_s", [128, 512])
    WTp = sb("WTp", [128, 4, 128], BF16); WTs = sb("WTs", [128, 4, 128], BF16)
    ident = sb("ident", [128, 128])
    onesm = sb("onesm", [128, 128], BF16)
    cT = sb("cT", [128, 76])
    RC = sb("RC", [128, 4, 16])
    misc = sb("misc", [128, 192])
    Csg = sb("Csg", [128, 4, 8])
    convst = sb("convst", [128, 4, 2])
    cvs_buf = sb("cvs_buf", [128, 4, 16, 2])
    poolst = sb("poolst", [128, NCH, 15])
    scT = sb("scT", [128, 4, 16, 2])
    tmp16 = sb("tmp16", [128, 16])
    banks = [es.enter_context(nc.psum_tensor(f"ps{i}", [128, 512], F32)) for i in range(8)]

    bigbf = big[:, 0:NF * TMAX // 2].bitcast(BF16)
    aT = bigbf.rearrange("p (f t) -> p f t", f=NF)
    vg_all = big[:, 0:4608].rearrange("p (i j) -> p i j", i=9)
    uT = big[:, 4608:9216].rearrange("p (c t) -> p c t", c=4)
    o = 9216
    xin_s = big[:, o:o + 1152]; o += 1152
    ext = big[:, o:o + 1028]; o += 1028
    ext_s = big[:, o:o + 160].rearrange("p (b k) -> p b k", b=16); o += 160
    gb_s = big[:, o:o + 1152]; o += 1152
    acc = big[:, o:o + 1152]; o += 1152
    assert o <= NBIG
    p_ext = big[:, 0:8320].rearrange("p (c t) -> p c t", c=8)
    ps_ext = big[:, 8320:8320 + 2944].rearrange("p (c b k) -> p c b k", c=8, b=16)
    o = 8320 + 2944
    t1 = big[:, o:o + 1040]; o += 1040
    t2 = big[:, o:o + 1040]; o += 1040
    ts1 = big[:, o:o + 368].rearrange("p (b k) -> p b k", b=16); o += 368
    ts2 = big[:, o:o + 368].rearrange("p (b k) -> p b k", b=16); o += 368
    assert o <= NBIG
    yT = big[:, 0:NCH * TMAX].rearrange("p (c t) -> p c t", c=NCH)

    stats = misc[:, 0:54].rearrange("p (i s) -> p i s", i=9)
    mv = misc[:, 54:72].rearrange("p (i s) -> p i s", i=9)
    rl = misc[:, 72:81]
    rl2 = misc[:, 81:90]
    ssq = misc[:, 128:146]
    fsum = misc[:, 146:155]
    fsq = misc[:, 155:164]
    frs = misc[:, 164:173]

    P = Prog()

    wstate = dict(n=0)

    def witem(loads):
        i = wstate["n"]
        wstate["n"] += 1
        if P.dry:
            P.witems.append(loads)
            return i % NSLOT
        j = i + NSLOT - 2
        if j < len(P.witems):
            _emit_wload(j)
        return i % NSLOT

    def _emit_wload(j, extra_r=()):
        loads = P.witems[j]
        s = j % NSLOT
        slot = wslots[s]

        def fn(e, loads=loads, slot=slot):
            res = []
            for (off, shp, src) in loads:
                n = 1
                for v in shp:
                    n *= v
                dst = slot[:, off:off + n]
                if len(shp) == 2:
                    dst = dst.rearrange("p (a b) -> p a b", a=shp[0])
                res.append(e.dma_start(out=dst, in_=src))
            return res
        P.add("pool", fn, r=list(extra_r), w=[("ws", s)], ndma=len(loads))

    def wprologue():
        if P.dry:
            return
        for j in range(min(NSLOT - 2, len(P.witems))):
            _emit_wload(j, extra_r=[("big", "load", "st", SBS[0]["T"] // 128 - 1)])

    def slot_view(s, off, a, b):
        return wslots[s][:, off:off + a * b].rearrange("p (a b) -> p a b", a=a)

    def emit_all():
        wstate["n"] = 0
        prologue_a()
        load_dma(SBS[0])
        wprologue()
        for S in SBS:
            load_rest(S)
            if S["idx"] == 0:
                prologue_b()
            tl = None
            for layer in range(2):
                if layer == 1 and S["Ts"]:
                    odd_prefetch(S)
                tl = ffn(S, layer, 0, (layer * 3 + 1, "h"), tl)
                if layer == 0:
                    tl = even_mixer(S, tl)
                else:
                    tl = odd_mixer(S, tl)
                tl = ffn(S, layer, 1, (3, "h") if layer == 0 else None, tl)
            if S["idx"] == 0:
                load_dma(SBS[1])
            final_out(S)
        P.add("sp", None, r=list(out_keys))

    out_keys = []

    def okey(name):
        k = ("out", name, len(out_keys))
        out_keys.append(k)
        return k

    def prologue_a():
        P.add("pool", lambda e: e.memset(stage[1][:, 0:128], 1.0), w=[("stage", 1)])
        P.add("pool", lambda e: e.affine_select(out=ident[:], in_=stage[1][:, 0:128], pattern=[[1, 128]],
                                                compare_op=ALU.is_equal, fill=0.0, base=0, channel_multiplier=-1),
              r=[("stage", 1)], w=[("ident",)])
        P.add("pool", lambda e: e.memset(onesm[:], 1.0 / 1024.0), w=[("onesm",)])
        P.add("pool", lambda e: e.iota(RC[:], pattern=[[0, 4], [1, 16]], base=1, channel_multiplier=0,
                                       allow_small_or_imprecise_dtypes=True), w=[("RC",)])
        for g in range(4):
            P.add("dve", lambda e, g=g: e.tensor_scalar_min(out=RC[:, g, :], in0=RC[:, g, :], scalar1=float(2 << g)),
                  r=[("RC",)], w=[("RC",)])
        P.add("dve", lambda e: e.reciprocal(out=RC[:], in_=RC[:]), r=[("RC",)], w=[("RC",)])
        def ld_c(e):
            return [e.dma_start(out=stage[0][0:48, 0:128], in_=ng),
                    e.dma_start(out=stage[0][48:56, 0:128], in_=fg),
                    e.dma_start(out=stage[0][56:68, 0:128], in_=convw),
                    e.dma_start(out=stage[0][68:76, 0:128], in_=pscale)]
        P.add("sp", ld_c, w=[("stage", 0)], ndma=4)
        P.add("pe", lambda e: e.transpose(banks[7][:, 0:76], stage[0][0:76, 0:128], ident[0:76, 0:76]),
              r=[("stage", 0), ("ident",)], w=[("ps", 7)])
        P.add("act", lambda e: e.activation(out=cT[:], in_=banks[7][:, 0:76], func=AF.Copy),
              r=[("ps", 7)], w=[("cT",)])
    def prologue_b():
        def ld_b(e):
            return [e.dma_start(out=lng_bc[:].unsqueeze(1), in_=lng.partition_broadcast(128)),
                    e.dma_start(out=lnb_bc[:].unsqueeze(1), in_=lnb.partition_broadcast(128)),
                    e.dma_start(out=bsb_p[:].unsqueeze(1), in_=sgub.rearrange("(o h) t -> o (h t)", o=1).partition_broadcast(128))]
        P.add("sp", ld_b, w=[("lnbc",), ("bs_p",)], ndma=3)
        for h in range(4):
            P.add("pool", lambda e, h=h: e.tensor_copy(
                out=bsb_s[:, h * 128:(h + 1) * 128].rearrange("p (b t) -> p b t", b=16),
                in_=bsb_p[:, h * 128:h * 128 + 8].unsqueeze(1).to_broadcast([128, 16, 8])),
                r=[("bs_p",)], w=[("bs_s",)])
        wn = stage[1][:, 0:512].rearrange("p (h s) -> p h s", h=4)
        P.add("sp", lambda e: [e.dma_start(out=wn, in_=sguw.rearrange("h t s -> t h s"))],
              r=[("ident",)], w=[("stage", 1)], ndma=1)
        P.add("pool", lambda e: e.affine_select(out=wn, in_=wn, pattern=[[0, 4], [-1, 128]], compare_op=ALU.is_ge,
                                                fill=0.0, base=0, channel_multiplier=1),
              r=[("stage", 1)], w=[("stage", 1)])
        def tr_p(e):
            for h in range(4):
                ins = e.transpose(banks[4][:, h * 128:(h + 1) * 128], wn[:, h, :], ident[:])
            return ins
        P.add("pe", tr_p, r=[("stage", 1), ("ident",)], w=[("ps", 4)])
        P.add("act", lambda e: e.activation(out=WTp[:], in_=banks[4][:].rearrange("p (h t) -> p h t", h=4), func=AF.Copy),
              r=[("ps", 4)], w=[("WTp",)])
        def ld_cs(e):
            res = []
            for j in range(16):
                res.append(e.dma_start(out=Csg[8 * j:8 * j + 8, :, :], in_=sguw[:, 0:8, 0:8].rearrange("h t s -> t h s")))
            return res
        P.add("sp", ld_cs, w=[("Csg",)], ndma=16)
        wns = stage[0][:, 512:1024].rearrange("p (h b s) -> p h b s", h=4, b=16)
        for h in range(4):
            P.add("pool", lambda e, h=h: e.affine_select(out=wns[:, h, :, :], in_=Csg[:, h, :].unsqueeze(1).to_broadcast([128, 16, 8]),
                                                    pattern=[[-8, 16], [-1, 8]], compare_op=ALU.is_ge, fill=0.0,
                                                    base=0, channel_multiplier=1),
                  r=[("Csg",), ("ps", 7)], w=[("stage", 0)])
        P.add("pool", lambda e: e.affine_select(out=wns, in_=wns, pattern=[[0, 4], [8, 16], [0, 8]], compare_op=ALU.is_ge,
                                                fill=0.0, base=7, channel_multiplier=-1),
              r=[("stage", 0)], w=[("stage", 0)])
        def tr_s(e):
            for h in range(4):
                ins = e.transpose(banks[5][:, h * 128:(h + 1) * 128],
                                  stage[0][:, 512 + h * 128:512 + (h + 1) * 128], ident[:])
            return ins
        P.add("pe", tr_s, r=[("stage", 0), ("ident",)], w=[("ps", 5)])
        P.add("act", lambda e: e.activation(out=WTs[:], in_=banks[5][:].rearrange("p (h t) -> p h t", h=4), func=AF.Copy),
              r=[("ps", 5)], w=[("WTs",)])

    def load_dma(S):
        nt = S["T"] // 128
        for i in range(nt):
            if i * 128 < S["Tp"]:
                src = xp[S["poff"] + i * 128:S["poff"] + (i + 1) * 128, :]
            else:
                src = xs
            dst = big[:, i * 1024:(i + 1) * 1024]
            q = "sp" if S["idx"] == 0 else "act"
            P.add(q, lambda e, dst=dst, src=src: [e.dma_start(out=dst, in_=src)], w=[("big", "load", "st", i)], ndma=1)

    def load_rest(S):
        nt = S["T"] // 128
        for i in range(nt):
            for half in range(2):
                bk = (i % 3) * 2 + half
                def tr(e, i=i, half=half, bk=bk):
                    for k in range(4):
                        c = half * 4 + k
                        ins = e.transpose(banks[bk][:, k * 128:(k + 1) * 128],
                                          big[:, i * 1024 + c * 128:i * 1024 + (c + 1) * 128], ident[:])
                    return ins
                P.add("pe", tr, r=[("big", "load", "st", i), ("ident",)], w=[("ps", bk)])
                dst = xT[:, half * 4:half * 4 + 4, i * 128:(i + 1) * 128]
                srcp = banks[bk][:].rearrange("p (c t) -> p c t", c=4)
                wk = K("x", range(half * 4, half * 4 + 4), i * 128, 128)
                if half == 0:
                    P.add("act", lambda e, dst=dst, srcp=srcp: e.activation(out=dst, in_=srcp, func=AF.Copy),
                          r=[("ps", bk)], w=wk)
                else:
                    P.add("dve", lambda e, dst=dst, srcp=srcp: e.tensor_copy(out=dst, in_=srcp),
                          r=[("ps", bk)], w=wk)
        for bi in range(len(S["blocks"])):
            for dc in range(NCH):
                stats_mm(S, stats_sq(S, dc, bi))
            norm_block(S, bi, (0, "h"))
        if S["Ts"]:
            P.add("sp", lambda e: [e.dma_start(out=stage[0][0:32, 0:512], in_=sc)], w=[("stage", 0)], ndma=1)
            def tr_sc(e):
                for cc in range(4):
                    ins = e.transpose(banks[4][:, cc * 32:(cc + 1) * 32], stage[0][0:32, cc * 128:(cc + 1) * 128],
                                      ident[0:32, 0:32])
                return ins
            P.add("pe", tr_sc, r=[("stage", 0), ("ident",)], w=[("ps", 4)])
            P.add("act", lambda e: e.activation(out=scT[:].rearrange("p c b r -> p c (b r)"),
                                                in_=banks[4][:, 0:128].rearrange("p (c n) -> p c n", c=4), func=AF.Copy),
                  r=[("ps", 4)], w=[("scT",)])

    def sbank(S, bi):
        return 6 if bi == len(S["blocks"]) - 1 else (1, 3)[bi]


    def preswitch():
        P.add("act", lambda e: e.activation(out=misc[:, 120:121], in_=onesm[:, 0:1], func=AF.Ln), r=[("onesm",)], w=[("junk",)])

    def stats_sq(S, dc, bi):
        a, n = S["blocks"][bi]
        nb = len(S["blocks"])
        k = cnt["sq"] % (2 * nb)
        cnt["sq"] += 1
        buf = sqbuf[:, k * n:(k + 1) * n]
        P.add("act", lambda e, dc=dc, a=a, n=n, buf=buf: e.activation(out=buf, in_=xT[:, dc, a:a + n], func=AF.Square),
              r=K("x", dc, a, n), w=[("sqb", k)])
        return (dc, bi, k)

    def stats_mm(S, item):
        dc, bi, k = item
        a, n = S["blocks"][bi]
        buf = sqbuf[:, k * n:(k + 1) * n]
        bank = banks[sbank(S, bi)]
        P.add("pe", lambda e, dc=dc, n=n, buf=buf, bank=bank: e.matmul(bank[:, 0:n], onesm[:], buf, start=(dc == 0),
                                                                     stop=(dc == NCH - 1)),
              r=[("sqb", k), ("onesm",)], w=[("ps", sbank(S, bi))])

    def norm_block(S, bi, nxt, defer=None):
        nidx, out = nxt
        if True:
            a, n = S["blocks"][bi]
            bk = sbank(S, bi)
            while defer:
                defer.pop(0)()
            P.add("act", lambda e, n=n, bk=bk: e.activation(out=rt[:, 0:n], in_=banks[bk][:, 0:n], func=AF.Ln, bias=EPS, scale=1.0),
                  r=[("ps", bk)], w=[("rt",)])
            P.add("act", lambda e, n=n: e.activation(out=rs[:, 0:n], in_=rt[:, 0:n], func=AF.Exp, scale=-0.5),
                  r=[("rt",)], w=[("rs",)])
            thunks = []
            for c in range(NCH):
                if out == "h":
                    dst = hT[:, c, a:a + n]; wk = K("h", c, a, n)
                else:
                    dst = yT[:, c, a:a + n]; wk = [("big", "final", "y", c, t) for t in tiles_of(a, n)]
                def th(dst=dst, c=c, a=a, n=n, wk=wk):
                    P.add("dve", lambda e: e.scalar_tensor_tensor(
                        out=dst, in0=xT[:, c, a:a + n], scalar=cT[:, nidx * 8 + c:nidx * 8 + c + 1], in1=rs[:, 0:n],
                        op0=ALU.mult, op1=ALU.mult),
                        r=K("x", c, a, n) + [("rs",), ("cT",)], w=wk)
                thunks.append(th)
            if defer is None:
                for th in thunks:
                    th()
            else:
                defer.extend(thunks)

    cnt = dict(a=0, b=0, m=0, sq=0, ft=0)

    def ffn(S, layer, which, nxt, prev_tail=None):
        wgv = wg[layer, which].rearrange("(c p) f -> p c f", p=128)
        wuv = wu[layer, which].rearrange("(c p) f -> p c f", p=128)
        wdv = wd[layer, which].rearrange("(f p) d -> p f d", p=128)
        def gu_group(s, f, a, n):
            sg = slot_view(s, 0, 8, 256); su = slot_view(s, 2048, 8, 256)
            fi = f % 2
            k = cnt["a"] % 2
            cnt["a"] += 1
            pg = banks[k]; pu = banks[2 + k]
            def mm(e, sg=sg, su=su, fi=fi, a=a, n=n, pg=pg, pu=pu):
                for c in range(NCH):
                    e.matmul(pg[:, 0:n], sg[:, c, fi * 128:(fi + 1) * 128], hT[:, c, a:a + n],
                             start=(c == 0), stop=(c == NCH - 1))
                for c in range(NCH):
                    ins = e.matmul(pu[:, 0:n], su[:, c, fi * 128:(fi + 1) * 128], hT[:, c, a:a + n],
                                   start=(c == 0), stop=(c == NCH - 1))
                return ins
            P.add("pe", mm, r=K("h", ALLC, a, n) + [("ws", s)], w=[("ps", k), ("ps", 2 + k)])
            P.add("act", lambda e, k=k, n=n, pg=pg: e.activation(out=stmp[k][:, 0:n], in_=pg[:, 0:n], func=AF.Silu),
                  r=[("ps", k)], w=[("stmp", k)])
            P.add("dve", lambda e, k=k, f=f, a=a, n=n, pu=pu: e.tensor_tensor(
                out=aT[:, f, a:a + n], in0=pu[:, 0:n], in1=stmp[k][:, 0:n], op=ALU.mult),
                r=[("ps", 2 + k), ("stmp", k)], w=[("big", "ffn", "a", f, t) for t in tiles_of(a, n)])

        def wpair(f2):
            return witem([(0, (8, 256), wgv[:, :, f2 * 256:(f2 + 1) * 256]),
                          (2048, (8, 256), wuv[:, :, f2 * 256:(f2 + 1) * 256])])

        s0 = wpair(0); s1 = wpair(1)
        for bi_, (a, n) in enumerate(S["blocks"]):
            for f in range(4):
                gu_group(s0 if f < 2 else s1, f, a, n)
                if bi_ == 0 and f == 1 and prev_tail is not None:
                    prev_tail()
        for f2 in range(2, NF // 2):
            s = wpair(f2)
            for (a, n) in S["blocks"]:
                for fi in range(2):
                    gu_group(s, f2 * 2 + fi, a, n)
        if nxt is not None:
            preswitch()
        else:
            final_prefetch()

        def down_group(s, dc, bi, mid=None):
            a, n = S["blocks"][bi]
            sd = slot_view(s, 0, NF, 128)
            k = 4 + cnt["b"] % 2
            cnt["b"] += 1
            po = banks[k]
            rk = [("big", "ffn", "a", f, t) for f in range(NF) for t in tiles_of(a, n)] + [("ws", s)]
            parts = [(0, NF)] if mid is None else [(0, NF // 2), (NF // 2, NF)]
            for pi, (f0, f1) in enumerate(parts):
                def mm(e, sd=sd, a=a, n=n, po=po, f0=f0, f1=f1):
                    for f in range(f0, f1):
                        ins = e.matmul(po[:, 0:n], sd[:, f, :], aT[:, f, a:a + n], start=(f == 0), stop=(f == NF - 1))
                    return ins
                P.add("pe", mm, r=rk, w=[("ps", k)])
                if mid is not None and pi == 0:
                    mid()
            P.add("dve", lambda e, dc=dc, a=a, n=n, po=po: e.scalar_tensor_tensor(
                out=xT[:, dc, a:a + n], in0=po[:, 0:n], scalar=0.5, in1=xT[:, dc, a:a + n], op0=ALU.mult, op1=ALU.add),
                r=[("ps", k)] + K("x", dc, a, n), w=K("x", dc, a, n))
            return stats_sq(S, dc, bi) if nxt is not None else None

        nb = len(S["blocks"])
        pending = []
        for dc in range(NCH - 2):
            s = witem([(0, (NF, 128), wdv[:, :, dc * 128:(dc + 1) * 128])])
            new = [down_group(s, dc, bi) for bi in range(nb)]
            if nxt is not None:
                for it in pending:
                    stats_mm(S, it)
            pending = new
        s6 = witem([(0, (NF, 128), wdv[:, :, 6 * 128:7 * 128])])
        s7 = witem([(0, (NF, 128), wdv[:, :, 7 * 128:8 * 128])])
        defer = []
        for bi in range(nb):
            def mid(bi=bi):
                for it in pending:
                    stats_mm(S, it)
                if bi > 0:
                    norm_block(S, bi - 1, nxt, defer)
                    for _ in range(min(5, len(defer))):
                        defer.pop(0)()
            i6 = down_group(s6, 6, bi, mid if nxt is not None else None)
            for _ in range(min(4, len(defer))):
                defer.pop(0)()
            if nxt is not None:
                pass
            elif bi > 0:
                for i in tiles_of(*S["blocks"][bi - 1]):
                    final_tile(S, i, ((0, 1), (2, 3)))
            i7 = down_group(s7, 7, bi)
            for _ in range(min(4, len(defer))):
                defer.pop(0)()
            pending = [i6, i7]
        while defer:
            defer.pop(0)()

        def tail(pending=pending):
            if nxt is not None:
                for it in pending:
                    stats_mm(S, it)
                norm_block(S, nb - 1, nxt)
        return tail

    def proj_add(S, wv, src, srcname, nxt):
        preswitch()

        def proj_group(s, dc, bi):
            a, n = S["blocks"][bi]
            di = dc % 4
            sw = slot_view(s, (di // 2) * 2048, 8, 256)
            k = 4 + cnt["b"] % 2
            cnt["b"] += 1
            po = banks[k]
            def mm(e, sw=sw, di=di, a=a, n=n, po=po):
                for c in range(NCH):
                    ins = e.matmul(po[:, 0:n], sw[:, c, (di % 2) * 128:(di % 2 + 1) * 128], src[:, c, a:a + n],
                                   start=(c == 0), stop=(c == NCH - 1))
                return ins
            P.add("pe", mm, r=K(srcname, ALLC, a, n) + [("ws", s)], w=[("ps", k)])
            P.add("dve", lambda e, dc=dc, a=a, n=n, po=po: e.tensor_tensor(
                out=xT[:, dc, a:a + n], in0=po[:, 0:n], in1=xT[:, dc, a:a + n], op=ALU.add),
                r=[("ps", k)] + K("x", dc, a, n), w=K("x", dc, a, n))
            return stats_sq(S, dc, bi)

        nb = len(S["blocks"])
        sl = [witem([(0, (8, 256), wv[:, :, d2 * 512:d2 * 512 + 256]),
                     (2048, (8, 256), wv[:, :, d2 * 512 + 256:d2 * 512 + 512])]) for d2 in range(2)]
        fifo = []
        defer = []

        def pop():
            it = fifo.pop(0)
            stats_mm(S, it)
            if it[0] == NCH - 1:
                norm_block(S, it[1], nxt, defer)

        for bi in range(nb):
            for dc in range(NCH):
                fifo.append(proj_group(sl[dc // 4], dc, bi))
                for _ in range(min(2, len(defer))):
                    defer.pop(0)()
                if len(fifo) > 2:
                    pop()
        while defer:
            defer.pop(0)()

        def tail():
            while fifo:
                pop()
            while defer:
                defer.pop(0)()
        return tail

    def transpose_rows_out(srcs, nrows, keys, bank_ids=(7, 5)):
        nchk = len(srcs)
        nb = (nchk + 3) // 4
        for b in range(nb):
            bk = bank_ids[b]
            def tr(e, b=b, bk=bk):
                for k in range(min(4, nchk - 4 * b)):
                    ins = e.transpose(banks[bk][0:nrows, k * 128:(k + 1) * 128], srcs[4 * b + k], ident[:])
                return ins
            P.add("pe", tr, r=list(keys) + [("ident",)], w=[("ps", bk)])
            w_ = min(4, nchk - 4 * b) * 128
            P.add("act", lambda e, b=b, bk=bk, w_=w_: e.activation(out=stage[1][0:nrows, b * 512:b * 512 + w_],
                                                                    in_=banks[bk][0:nrows, 0:w_], func=AF.Copy),
                  r=[("ps", bk)], w=[("stage", 1)])

    def even_mixer(S, prev_tail):
        Tp, Ts, T = S["Tp"], S["Ts"], S["T"]
        nt = T // 128
        R = "even"
        if prev_tail is not None:
            prev_tail()
        s = witem([(0, (8, 512), ewin_v[:, :, 512:1024])])
        sv = slot_view(s, 0, 8, 512)
        for i in range(nt):
            k = cnt["m"] % 2
            cnt["m"] += 1
            pv = banks[k]
            def mm(e, i=i, pv=pv):
                for c in range(NCH):
                    ins = e.matmul(pv[:, :], hT[:, c, i * 128:(i + 1) * 128], sv[:, c, :], start=(c == 0), stop=(c == NCH - 1))
                return ins
            P.add("pe", mm, r=K("h", ALLC, i * 128, 128) + [("ws", s)], w=[("ps", k)])
            P.add("act", lambda e, i=i, pv=pv: e.activation(out=vg_all[:, i, :], in_=pv[:, :], func=AF.Gelu),
                  r=[("ps", k)], w=[("big", R, "vg", i)])
            P.add("dve", lambda e, i=i: e.bn_stats(out=stats[:, i, :], in_=vg_all[:, i, :]),
                  r=[("big", R, "vg", i)], w=[("stats", i)])
            P.add("dve", lambda e, i=i: e.bn_aggr(out=mv[:, i, :], in_=stats[:, i, :]), r=[("stats", i)], w=[("mv", i)])
        P.add("act", lambda e: e.activation(out=rl2[:, 0:nt], in_=mv[:, 0:nt, 1], func=AF.Sqrt, bias=EPS, scale=1.0),
              r=[("mv", i) for i in range(nt)], w=[("rl2",)])
        P.add("dve", lambda e: e.reciprocal(out=rl[:, 0:nt], in_=rl2[:, 0:nt]), r=[("rl2",)], w=[("rl",)])
        nmr = misc[:, 90:99]
        P.add("dve", lambda e: e.scalar_tensor_tensor(out=nmr[:, 0:nt], in0=mv[:, 0:nt, 0], scalar=-1.0, in1=rl[:, 0:nt],
                                                      op0=ALU.mult, op1=ALU.mult),
              r=[("mv", i) for i in range(nt)] + [("rl",)], w=[("nmr",)])
        ln_thunks = []

        def ln_tile(i):
            is_s = (i * 128 >= Tp)
            vg = vg_all[:, i, :]
            P.add("act", lambda e, i=i, vg=vg: e.activation(out=vg, in_=vg, func=AF.Identity, scale=rl[:, i:i + 1],
                                                           bias=nmr[:, i:i + 1]),
                  r=[("big", R, "vg", i), ("nmr",), ("rl",)], w=[("big", R, "vg", i)])
            P.add("dve", lambda e, vg=vg: e.tensor_tensor(out=vg, in0=vg, in1=lng_bc[:], op=ALU.mult),
                  r=[("big", R, "vg", i), ("lnbc",)], w=[("big", R, "vg", i)])
            if not is_s:
                P.add("dve", lambda e, vg=vg, i=i: e.tensor_tensor(out=vbf_all[:, i, :], in0=vg, in1=lnb_bc[:], op=ALU.add),
                      r=[("big", R, "vg", i), ("lnbc",)], w=[("vbf", i)])
            else:
                P.add("dve", lambda e, vg=vg: e.tensor_tensor(out=vfp[:], in0=vg, in1=lnb_bc[:], op=ALU.add),
                      r=[("big", R, "vg", i), ("lnbc",)], w=[("stmp", 0)])
                P.add("dve", lambda e, i=i: e.tensor_copy(out=vbf_all[:, i, :], in_=vfp[:]),
                      r=[("stmp", 0)], w=[("vbf", i)])
                P.add("sp", lambda e: [e.dma_start(out=cvv, in_=vfp[:])], r=[("stmp", 0)], w=[okey("cvv")], ndma=1)

        ln_thunks = [(lambda i=i: ln_tile(i)) for i in range(nt)]
        ln_thunks.pop(0)()
        s = witem([(0, (8, 256), ewin_v[:, :, 0:256]), (2048, (8, 256), ewin_v[:, :, 256:512])])
        for j in range(4):
            sw = slot_view(s, (j // 2) * 2048, 8, 256)
            for (a, n) in S["blocks"]:
                k = 2 + cnt["m"] % 2
                cnt["m"] += 1
                pb = banks[k]
                def mm(e, sw=sw, j=j, a=a, n=n, pb=pb):
                    for c in range(NCH):
                        ins = e.matmul(pb[:, 0:n], sw[:, c, (j % 2) * 128:(j % 2 + 1) * 128], hT[:, c, a:a + n],
                                       start=(c == 0), stop=(c == NCH - 1))
                    return ins
                P.add("pe", mm, r=K("h", ALLC, a, n) + [("ws", s)], w=[("ps", k)])
                P.add("act", lambda e, j=j, a=a, n=n, pb=pb: e.activation(out=uT[:, j, a:a + n], in_=pb[:, 0:n], func=AF.Gelu),
                      r=[("ps", k)], w=[("big", R, "u", j, t) for t in tiles_of(a, n)])
                if ln_thunks:
                    ln_thunks.pop(0)()
        while ln_thunks:
            ln_thunks.pop(0)()
        for i in range(nt):
            is_s = (i * 128 >= Tp)
            WT = WTs if is_s else WTp
            bsb = bsb_s if is_s else bsb_p
            wkey = ("WTs",) if is_s else ("WTp",)
            bkey = ("bs_s",) if is_s else ("bs_p",)
            k = cnt["m"] % 2
            cnt["m"] += 1
            py = banks[k]
            def mm(e, i=i, WT=WT, py=py):
                for hh in range(4):
                    ins = e.matmul(py[:, hh * 128:(hh + 1) * 128], vbf_all[:, i, hh * 128:(hh + 1) * 128], WT[:, hh, :],
                                   start=True, stop=True)
                return ins
            P.add("pe", mm, r=[("vbf", i), wkey], w=[("ps", k)])
            tb_ = rt if i % 2 == 0 else rs
            tk = ("rt",) if i % 2 == 0 else ("rs",)
            P.add("dve", lambda e, py=py, tb_=tb_, bsb=bsb: e.tensor_tensor(out=tb_[:], in0=py[:], in1=bsb[:], op=ALU.add),
                  r=[("ps", k), bkey], w=[tk])
            P.add("pool", lambda e, i=i, tb_=tb_: e.tensor_tensor(
                out=mixT[:, 0:4, i * 128:(i + 1) * 128], in0=tb_[:].rearrange("p (h t) -> p h t", h=4),
                in1=uT[:, 0:4, i * 128:(i + 1) * 128], op=ALU.mult),
                r=[tk] + [("big", R, "u", j, i) for j in range(4)], w=K("mix", range(4), i * 128, 128))
        for pp in range(2):
            s1 = witem([(0, (8, 256), ewin_v[:, :, 1536 + pp * 256:1536 + (pp + 1) * 256]),
                        (2048, (8, 256), ewin_v[:, :, 2048 + pp * 256:2048 + (pp + 1) * 256])])
            s2 = witem([(0, (8, 256), ewin_v[:, :, 1024 + pp * 256:1024 + (pp + 1) * 256])])
            sc_ = slot_view(s1, 0, 8, 256); sx_ = slot_view(s1, 2048, 8, 256); sb_ = slot_view(s2, 0, 8, 256)
            for ci in range(2):
                cc = pp * 2 + ci
                csl = slice(ci * 128, (ci + 1) * 128)
                w0 = cT[:, 56 + cc:57 + cc]; w1 = cT[:, 60 + cc:61 + cc]; w2 = cT[:, 64 + cc:65 + cc]
                extk = [("big", R, "ext0")] + [("big", R, "ext", t) for t in range(Tp // 128)]
                if S["idx"] == 0:
                    P.add("dve", lambda e: e.memset(ext[:, 0:2], 0.0), w=[("big", R, "ext0")])
                else:
                    P.add("dve", lambda e, cc=cc: e.tensor_copy(out=ext[:, 0:2], in_=convst[:, cc, :]),
                          r=[("convst", cc)], w=[("big", R, "ext0")])
                    P.add("dve", lambda e, cc=cc: e.tensor_copy(out=ext_s[:, :, 0:2], in_=scT[:, cc, :, :]),
                          r=[("scT",)], w=[("big", R, "exts0")])
                for (a, n) in S["blocks"]:
                    pn = max(0, min(a + n, Tp) - a)
                    sn = n - pn
                    k = cnt["m"] % 2
                    cnt["m"] += 1
                    pc = banks[k]; px = banks[2 + k]
                    def mm(e, a=a, n=n, pc=pc, px=px, csl=csl, sc_=sc_, sx_=sx_):
                        for c in range(NCH):
                            e.matmul(pc[:, 0:n], sc_[:, c, csl], hT[:, c, a:a + n], start=(c == 0), stop=(c == NCH - 1))
                        for c in range(NCH):
                            ins = e.matmul(px[:, 0:n], sx_[:, c, csl], hT[:, c, a:a + n], start=(c == 0), stop=(c == NCH - 1))
                        return ins
                    P.add("pe", mm, r=K("h", ALLC, a, n) + [("ws", s1)], w=[("ps", k), ("ps", 2 + k)])
                    P.add("act", lambda e, a=a, n=n, px=px: e.activation(out=xin_s[:, a:a + n], in_=px[:, 0:n], func=AF.Copy),
                          r=[("ps", 2 + k)], w=[("big", R, "xin", t) for t in tiles_of(a, n)])
                    if pn:
                        P.add("dve", lambda e, a=a, pn=pn, pc=pc: e.tensor_tensor(
                            out=ext[:, 2 + a:2 + a + pn], in0=pc[:, 0:pn], in1=xin_s[:, a:a + pn], op=ALU.mult),
                            r=[("ps", k)] + [("big", R, "xin", t) for t in tiles_of(a, pn)],
                            w=[("big", R, "ext", t) for t in tiles_of(a, pn)])
                    if sn:
                        P.add("dve", lambda e, a=a, pn=pn, sn=sn, pc=pc: e.tensor_tensor(
                            out=ext_s[:, :, 2:10], in0=pc[:, pn:pn + sn].rearrange("p (b k) -> p b k", b=16),
                            in1=xin_s[:, a + pn:a + pn + sn].rearrange("p (b k) -> p b k", b=16), op=ALU.mult),
                            r=[("ps", k)] + [("big", R, "xin", t) for t in tiles_of(a + pn, sn)],
                            w=[("big", R, "exts")])
                    k2 = 4 + cnt["b"] % 2
                    cnt["b"] += 1
                    pb = banks[k2]
                    def mm2(e, a=a, n=n, pb=pb, csl=csl, sb_=sb_):
                        for c in range(NCH):
                            ins = e.matmul(pb[:, 0:n], sb_[:, c, csl], hT[:, c, a:a + n], start=(c == 0), stop=(c == NCH - 1))
                        return ins
                    P.add("pe", mm2, r=K("h", ALLC, a, n) + [("ws", s2)], w=[("ps", k2)])
                    P.add("act", lambda e, a=a, n=n, pb=pb: e.activation(out=gb_s[:, a:a + n], in_=pb[:, 0:n], func=AF.Copy),
                          r=[("ps", k2)], w=[("big", R, "gb", t) for t in tiles_of(a, n)])
                    if pn:
                        ek = [("big", R, "ext0")] + [("big", R, "ext", t) for t in range((a + pn) // 128)]
                        ak = ("big", R, "acc", a)
                        P.add("dve", lambda e, w0=w0, a=a, pn=pn: e.tensor_scalar(
                            out=acc[:, a:a + pn], in0=ext[:, a:a + pn], scalar1=w0, scalar2=None, op0=ALU.mult),
                            r=ek + [("cT",)], w=[ak])
                        P.add("dve", lambda e, w1=w1, a=a, pn=pn: e.scalar_tensor_tensor(
                            out=acc[:, a:a + pn], in0=ext[:, a + 1:a + pn + 1], scalar=w1, in1=acc[:, a:a + pn],
                            op0=ALU.mult, op1=ALU.add), r=ek + [ak], w=[ak])
                        P.add("dve", lambda e, w2=w2, a=a, pn=pn: e.scalar_tensor_tensor(
                            out=acc[:, a:a + pn], in0=ext[:, a + 2:a + pn + 2], scalar=w2, in1=acc[:, a:a + pn],
                            op0=ALU.mult, op1=ALU.add), r=ek + [ak], w=[ak])
                        P.add("dve", lambda e, cc=cc, a=a, pn=pn: e.tensor_tensor(
                            out=mixT[:, 4 + cc, a:a + pn], in0=acc[:, a:a + pn], in1=gb_s[:, a:a + pn], op=ALU.mult),
                            r=[ak] + [("big", R, "gb", t) for t in tiles_of(a, pn)], w=K("mix", 4 + cc, a, pn))
                P.add("dve", lambda e, cc=cc: e.tensor_copy(out=convst[:, cc, :], in_=ext[:, Tp:Tp + 2]),
                      r=extk, w=[("convst", cc)])
                if Ts:
                    accs = acc[:, Tp:Tp + 128].rearrange("p (b k) -> p b k", b=16)
                    gbs = gb_s[:, Tp:Tp + 128].rearrange("p (b k) -> p b k", b=16)
                    exk = [("big", R, "exts0"), ("big", R, "exts")]
                    P.add("dve", lambda e, w0=w0, accs=accs: e.tensor_scalar(out=accs, in0=ext_s[:, :, 0:8], scalar1=w0,
                                                                            scalar2=None, op0=ALU.mult),
                          r=exk + [("cT",)], w=[("big", R, "accs")])
                    P.add("dve", lambda e, w1=w1, accs=accs: e.scalar_tensor_tensor(
                        out=accs, in0=ext_s[:, :, 1:9], scalar=w1, in1=accs, op0=ALU.mult, op1=ALU.add),
                        r=exk + [("big", R, "accs")], w=[("big", R, "accs")])
                    P.add("dve", lambda e, w2=w2, accs=accs: e.scalar_tensor_tensor(
                        out=accs, in0=ext_s[:, :, 2:10], scalar=w2, in1=accs, op0=ALU.mult, op1=ALU.add),
                        r=exk + [("big", R, "accs")], w=[("big", R, "accs")])
                    P.add("dve", lambda e, cc=cc, accs=accs, gbs=gbs: e.tensor_tensor(
                        out=mixT[:, 4 + cc, Tp:Tp + 128].rearrange("p (b k) -> p b k", b=16), in0=accs, in1=gbs, op=ALU.mult),
                        r=[("big", R, "accs"), ("big", R, "gb", Tp // 128)], w=K("mix", 4 + cc, Tp, 128))
                    P.add("dve", lambda e, cc=cc: e.tensor_copy(out=cvs_buf[:, cc, :, :], in_=ext_s[:, :, 8:10]),
                          r=exk, w=[("cvs_buf", cc)])
        tail_ = proj_add(S, ewout_v, mixT, "mix", (2, "h"))
        if S["idx"] == 1:
            transpose_rows_out([convst[:, c, :] for c in range(4)], 2, [("convst", c) for c in range(4)])
            P.add("sp", lambda e: [e.dma_start(out=cvp, in_=stage[1][0:2, 0:512])], r=[("stage", 1)], w=[okey("cvp")], ndma=1)
            transpose_rows_out([cvs_buf[:, c, :, :].rearrange("p b r -> p (b r)") for c in range(4)], 32,
                               [("cvs_buf", c) for c in range(4)])
            P.add("sp", lambda e: [e.dma_start(out=cvs, in_=stage[1][0:32, 0:512])], r=[("stage", 1)], w=[okey("cvs")], ndma=1)
        return tail_

    def odd_prefetch(S):
        for hb in range(2):
            P.add("sp", lambda e, hb=hb: [e.dma_start(out=stage[hb][0:120, :], in_=spl[hb * 120:(hb + 1) * 120, :])],
                  w=[("stage", hb)], ndma=1)

    def odd_mixer(S, prev_tail):
        Tp, Ts, T = S["Tp"], S["Ts"], S["T"]
        R = "odd"
        L = 15 + Tp
        if prev_tail is not None:
            prev_tail()
        if S["idx"] == 0:
            P.add("dve", lambda e: e.memset(p_ext[:, :, 0:15], 0.0), w=[("big", R, "p0")])
        else:
            P.add("dve", lambda e: e.tensor_copy(out=p_ext[:, :, 0:15], in_=poolst[:]),
                  r=[("poolst", c) for c in range(NCH)], w=[("big", R, "p0")])
            for hb in range(2):
                for half in range(2):
                    bk = (7, 5)[half]
                    def tr(e, half=half, bk=bk, hb=hb):
                        for k in range(4):
                            c = half * 4 + k
                            ins = e.transpose(banks[bk][:, k * 128:k * 128 + 120], stage[hb][0:120, c * 128:(c + 1) * 128],
                                              ident[0:120, 0:120])
                        return ins
                    P.add("pe", tr, r=[("stage", hb), ("ident",)], w=[("ps", bk)])
                    for k in range(4):
                        c = half * 4 + k
                        P.add("act", lambda e, c=c, k=k, bk=bk, hb=hb: e.activation(
                            out=ps_ext[:, c, hb * 8:(hb + 1) * 8, 0:15],
                            in_=banks[bk][:, k * 128:k * 128 + 120].rearrange("p (b r) -> p b r", b=8), func=AF.Copy),
                            r=[("ps", bk)], w=[("big", R, "ps0", c, hb)])
            P.add("sp", lambda e: [e.dma_start(out=pls[:, 0:7, :], in_=spl.rearrange("(b r) d -> b r d", r=15)[:, 8:15, :])],
                  w=[okey("pls_a")], ndma=1)
        for d2 in (1, 0):
            s = witem([(0, (8, 256), owin_v[:, :, d2 * 512:d2 * 512 + 256]),
                       (2048, (8, 256), owin_v[:, :, d2 * 512 + 256:d2 * 512 + 512])])
            for di in (2, 3, 0, 1):
                dc = d2 * 4 + di
                sw = slot_view(s, (di // 2) * 2048, 8, 256)
                for (a, n) in S["blocks"]:
                    pn = max(0, min(a + n, Tp) - a)
                    sn = n - pn
                    k = cnt["m"] % 2
                    cnt["m"] += 1
                    pb = banks[k]
                    def mm(e, sw=sw, di=di, a=a, n=n, pb=pb):
                        for c in range(NCH):
                            ins = e.matmul(pb[:, 0:n], sw[:, c, (di % 2) * 128:(di % 2 + 1) * 128], hT[:, c, a:a + n],
                                           start=(c == 0), stop=(c == NCH - 1))
                        return ins
                    P.add("pe", mm, r=K("h", ALLC, a, n) + [("ws", s)], w=[("ps", k)])
                    if pn:
                        P.add("act", lambda e, dc=dc, a=a, pn=pn, pb=pb: e.activation(
                            out=p_ext[:, dc, 15 + a:15 + a + pn], in_=pb[:, 0:pn], func=AF.Copy),
                            r=[("ps", k)], w=[("big", R, "p", dc, t) for t in tiles_of(a, pn)])
                    if sn:
                        P.add("act", lambda e, dc=dc, pn=pn, sn=sn, pb=pb: e.activation(
                            out=ps_ext[:, dc, :, 15:23], in_=pb[:, pn:pn + sn].rearrange("p (b k) -> p b k", b=16), func=AF.Copy),
                            r=[("ps", k)], w=[("big", R, "psn", dc)])
                g = dc // 2
                w_ = 2 << g
                E0 = p_ext[:, dc, :]
                pk = [("big", R, "p0")] + [("big", R, "p", dc, t) for t in range(Tp // 128)]
                chain = [(t1, E0, 1), (t2, t1, 2), (t1, t2, 4), (t2, t1, 8)][:g + 1]
                tk = {id(t1): ("big", R, "t1"), id(t2): ("big", R, "t2")}
                for (dst, src, sh) in chain:
                    lo = 2 * sh - 1
                    rk = pk if src is E0 else [tk[id(src)]]
                    P.add("dve", lambda e, dst=dst, src=src, sh=sh, lo=lo: e.tensor_tensor(
                        out=dst[:, lo:L], in0=src[:, lo:L], in1=src[:, lo - sh:L - sh], op=ALU.add),
                        r=rk, w=[tk[id(dst)]])
                Sw = chain[-1][0]
                P.add("dve", lambda e, dc=dc, Sw=Sw, w_=w_, E0=E0: e.scalar_tensor_tensor(
                    out=mixT[:, dc, 0:Tp], in0=Sw[:, 15:15 + Tp], scalar=1.0 / w_, in1=E0[:, 15:15 + Tp],
                    op0=ALU.mult, op1=ALU.subtract),
                    r=[tk[id(Sw)]] + pk, w=K("mix", dc, 0, Tp))
                if S["idx"] == 0:
                    P.add("dve", lambda e, Sw=Sw, g=g: e.tensor_tensor(out=tmp16[:], in0=Sw[:, 15:31], in1=RC[:, g, :], op=ALU.mult),
                          r=[tk[id(Sw)], ("RC",)], w=[("tmp16",)])
                    P.add("dve", lambda e, dc=dc, E0=E0: e.tensor_tensor(out=mixT[:, dc, 0:16], in0=tmp16[:], in1=E0[:, 15:31],
                                                                        op=ALU.subtract),
                          r=[("tmp16",)] + pk, w=K("mix", dc, 0, 16))
                P.add("act", lambda e, dc=dc, E0=E0: e.activation(out=poolst[:, dc, :], in_=E0[:, Tp:Tp + 15], func=AF.Copy),
                      r=pk, w=[("poolst", dc)])
                if Ts:
                    Es = ps_ext[:, dc, :, :]
                    sk = [("big", R, "ps0", dc, 0), ("big", R, "ps0", dc, 1), ("big", R, "psn", dc)]
                    chs = [(ts1, Es, 1), (ts2, ts1, 2), (ts1, ts2, 4), (ts2, ts1, 8)][:g + 1]
                    tks = {id(ts1): ("big", R, "ts1"), id(ts2): ("big", R, "ts2")}
                    for (dst, src, sh) in chs:
                        lo = 2 * sh - 1
                        rk = sk if src is Es else [tks[id(src)]]
                        P.add("dve", lambda e, dst=dst, src=src, sh=sh, lo=lo: e.tensor_tensor(
                            out=dst[:, :, lo:23], in0=src[:, :, lo:23], in1=src[:, :, lo - sh:23 - sh], op=ALU.add),
                            r=rk, w=[tks[id(dst)]])
                    Ss = chs[-1][0]
                    P.add("dve", lambda e, dc=dc, Ss=Ss, w_=w_, Es=Es: e.scalar_tensor_tensor(
                        out=mixT[:, dc, Tp:Tp + 128].rearrange("p (b k) -> p b k", b=16), in0=Ss[:, :, 15:23],
                        scalar=1.0 / w_, in1=Es[:, :, 15:23], op0=ALU.mult, op1=ALU.subtract),
                        r=[tks[id(Ss)]] + sk, w=K("mix", dc, Tp, 128))
        if S["idx"] == 1:
            transpose_rows_out([poolst[:, c, :] for c in range(NCH)], 15, [("poolst", c) for c in range(NCH)])
            P.add("sp", lambda e: [e.dma_start(out=plp, in_=stage[1][0:15, :])], r=[("stage", 1)], w=[okey("plp")], ndma=1)
            P.add("act", lambda e: e.activation(out=stage[0][:].rearrange("p (c b k) -> p c b k", c=8, b=16),
                                                in_=ps_ext[:, :, :, 15:23], func=AF.Copy),
                  r=[("big", R, "psn", c) for c in range(NCH)], w=[("stage", 0)])
            transpose_rows_out([stage[0][:, c * 128:(c + 1) * 128] for c in range(NCH)], 128, [("stage", 0)])
            P.add("sp", lambda e: [e.dma_start(out=pls[:, 7:15, :], in_=stage[1][:, :])], r=[("stage", 1)],
                  w=[okey("pls_b")], ndma=1)
        s = witem([(0, (8, 256), poolw_v)])
        spw = slot_view(s, 0, 8, 256)
        for oc in (6, 7, 4, 5, 2, 3, 0, 1):
            g = oc // 2
            osl = slice((oc % 2) * 128, (oc % 2 + 1) * 128)
            for (a, n) in S["blocks"]:
                k = 2 + cnt["m"] % 2
                cnt["m"] += 1
                pb = banks[k]
                def mm(e, g=g, osl=osl, a=a, n=n, pb=pb):
                    for kk in range(2):
                        ins = e.matmul(pb[:, 0:n], spw[:, 2 * g + kk, osl], mixT[:, 2 * g + kk, a:a + n],
                                       start=(kk == 0), stop=(kk == 1))
                    return ins
                P.add("pe", mm, r=K("mix", (2 * g, 2 * g + 1), a, n) + [("ws", s)], w=[("ps", k)])
                P.add("act", lambda e, oc=oc, a=a, n=n, pb=pb: e.activation(
                    out=hT[:, oc, a:a + n], in_=pb[:, 0:n], func=AF.Identity, scale=cT[:, 68 + oc:69 + oc]),
                    r=[("ps", k), ("cT",)], w=K("h", oc, a, n))
        return proj_add(S, owout_v, hT, "h", (5, "h"))

    def final_prefetch():
        fgv = fg.rearrange("(o c) p -> o (c p)", o=1)
        P.add("sp", lambda e: [e.dma_start(out=stmp[0][:].unsqueeze(1), in_=fgv[:, 0:512].partition_broadcast(128)),
                               e.dma_start(out=stmp[1][:].unsqueeze(1), in_=fgv[:, 512:1024].partition_broadcast(128))],
              w=[("stmp", 0), ("stmp", 1)], ndma=2)

    _fb = {}

    def final_bufs():
        if "b" not in _fb:
            hflat = hT[:].rearrange("p c t -> p (c t)").bitcast(F32)
            mflat = mixT[:].rearrange("p c t -> p (c t)").bitcast(F32)
            sb_ = [(stage[0][:], [("stage", 0)]), (stage[1][:], [("stage", 1)])]
            for j in range(4):
                sb_.append((hflat[:, j * 1024:(j + 1) * 1024], [("h", b // 9, b % 9) for b in range(16 * j, 16 * j + 16)]))
            for j in range(4):
                sb_.append((mflat[:, j * 1024:(j + 1) * 1024], [("mix", b // 9, b % 9) for b in range(16 * j, 16 * j + 16)]))
            _fb["b"] = sb_
        return _fb["b"]

    def final_tile(S, i, bank_pairs):
        sbufs = final_bufs()
        st, stk = sbufs[i % len(sbufs)]
        bk0 = bank_pairs[cnt["ft"] % len(bank_pairs)]
        cnt["ft"] += 1
        for half in range(2):
            bk = bk0[half]
            def tr(e, i=i, half=half, bk=bk):
                for k in range(4):
                    c = half * 4 + k
                    ins = e.transpose(banks[bk][:, k * 128:(k + 1) * 128], xT[:, c, i * 128:(i + 1) * 128], ident[:])
                return ins
            P.add("pe", tr, r=K("x", range(half * 4, half * 4 + 4), i * 128, 128) + [("ident",)], w=[("ps", bk)])
            junk = rt if half == 0 else rs
            jk = ("rt",) if half == 0 else ("rs",)
            P.add("act", lambda e, i=i, half=half, bk=bk, junk=junk: e.activation(
                out=junk[:], in_=banks[bk][:], func=AF.Square, accum_out=ssq[:, 2 * i + half:2 * i + half + 1]),
                r=[("ps", bk)], w=[jk, ("ssq", i, half)])
        P.add("act", lambda e, i=i: e.activation(out=fsum[:, i:i + 1], in_=ssq[:, 2 * i:2 * i + 1], func=AF.Identity,
                                                bias=EPS, scale=1.0 / D),
              r=[("ssq", i, 0)], w=[("fsum", i)])
        P.add("act", lambda e, i=i: e.activation(out=fsq[:, i:i + 1], in_=ssq[:, 2 * i + 1:2 * i + 2], func=AF.Sqrt,
                                                bias=fsum[:, i:i + 1], scale=1.0 / D),
              r=[("fsum", i), ("ssq", i, 1)], w=[("fsq", i)])
        P.add("dve", lambda e, i=i: e.reciprocal(out=frs[:, i:i + 1], in_=fsq[:, i:i + 1]), r=[("fsq", i)], w=[("frs", i)])
        for half in range(2):
            bk = bk0[half]
            P.add("dve", lambda e, i=i, half=half, bk=bk, st=st: e.scalar_tensor_tensor(
                out=st[:, half * 512:(half + 1) * 512], in0=banks[bk][:], scalar=frs[:, i:i + 1], in1=stmp[half][:],
                op0=ALU.mult, op1=ALU.mult),
                r=[("ps", bk), ("frs", i), ("stmp", half)], w=list(stk))
        if i * 128 < S["Tp"]:
            dst = yp[S["poff"] + i * 128:S["poff"] + (i + 1) * 128, :]
        else:
            dst = ys
        P.add("sp", lambda e, st=st, dst=dst: [e.dma_start(out=dst, in_=st)], r=list(stk),
              w=[okey("y")], ndma=1)

    def final_out(S):
        a, n = S["blocks"][-1]
        for i in tiles_of(a, n):
            final_tile(S, i, ((0, 1), (2, 3), (4, 5)))

    P.dry = True
    emit_all()
    P.dry = False
    del out_keys[:]
    cnt.update(a=0, b=0, m=0, sq=0, ft=0)
    emit_all()

    sem_names = {"pe": "s_pe", "act": "s_act", "dve": "s_dve", "pool": "s_pool", "sp": "s_sp"}
    sems = {k: es.enter_context(nc.semaphore(v)) for k, v in sem_names.items()}
    rings = {"sp": [es.enter_context(nc.semaphore(f"r_sp{i}")) for i in range(8)],
             "pool": [es.enter_context(nc.semaphore(f"r_pool{i}")) for i in range(8)],
             "act": [es.enter_context(nc.semaphore(f"r_act{i}")) for i in range(10)]}
    block = es.enter_context(nc.Block())
    P.emit(nc, block, sems, rings)
    _DBG["P"] = P
    es.close()
    return nc


_NC_CACHE = {}


def kernel(x_prompt, x_sample, state_conv, state_pool, norm_g, final_norm_g,
           ffn_w_gate, ffn_w_up, ffn_w_down, e_w_in, e_ln_g, e_ln_b, e_sgu_w, e_sgu_b,
           e_conv_w, e_w_out, o_w_in, o_pool_w, o_pool_scale, o_w_out):
    f = lambda a: np.ascontiguousarray(np.asarray(a, dtype=np.float32))
    if "nc" not in _NC_CACHE:
        _NC_CACHE["nc"] = build_program()
    nc = _NC_CACHE["nc"]
    x_prompt = f(x_prompt); x_sample = f(x_sample); state_conv = f(state_conv); state_pool = f(state_pool)
    shared = dict(
        ng=f(norm_g).reshape(48, 128), fg=f(final_norm_g).reshape(8, 128),
        wg=f(ffn_w_gate), wu=f(ffn_w_up), wd=f(ffn_w_down),
        ewin=f(e_w_in)[0], lng=f(e_ln_g).reshape(1, 512), lnb=f(e_ln_b).reshape(1, 512),
        sguw=f(e_sgu_w)[0], sgub=f(e_sgu_b)[0], convw=f(e_conv_w).reshape(12, 128), ewout=f(e_w_out)[0],
        owin=f(o_w_in)[0], poolw=f(o_pool_w)[0], pscale=f(o_pool_scale).reshape(8, 128), owout=f(o_w_out)[0],
    )
    in_maps = []
    for i in range(NCORES):
        m = dict(shared)
        m["xp"] = x_prompt[i]
        m["xs"] = x_sample[16 * i:16 * (i + 1)].reshape(128, D)
        m["sc"] = state_conv[0, 16 * i:16 * (i + 1)].reshape(32, 512)
        m["spl"] = state_pool[0, 16 * i:16 * (i + 1)].reshape(240, D)
        in_maps.append(m)
    res = run_bass_kernel_spmd(nc, in_maps, core_ids=list(range(NCORES)))
    R = res.results
    y_prompt = np.stack([R[i]["yp"] for i in range(NCORES)], 0).astype(np.float32)
    y_sample = np.concatenate([R[i]["ys"].reshape(16, 8, D) for i in range(NCORES)], 0).astype(np.float32)
    conv_prompt = np.stack([R[i]["cvp"] for i in range(NCORES)], 0)[None].astype(np.float32)
    conv_sample = np.concatenate([R[i]["cvs"].reshape(16, 2, 512) for i in range(NCORES)], 0)[None].astype(np.float32)
    chunk_v = np.concatenate([R[i]["cvv"].reshape(16, 8, 512) for i in range(NCORES)], 0)[None].astype(np.float32)
    pool_prompt = np.stack([R[i]["plp"] for i in range(NCORES)], 0)[None].astype(np.float32)
    pool_sample = np.concatenate([R[i]["pls"] for i in range(NCORES)], 0)[None].astype(np.float32)
    return (y_prompt, y_sample, conv_prompt, conv_sample, chunk_v, pool_prompt, pool_sample)
```
